# Optimizing a Trainium2 kernel written in Bass

```python
import math
import jax, jax.numpy as jnp
from jax import lax
import numpy as np

D_MODEL = 2048
BATCH = 4
SEQ = 8192
DEPTH = 4

CONV_WIDTH = D_MODEL // 2
CONV_KERNEL = 31
N_HEADS = 8
HEAD_DIM = D_MODEL // (4 * N_HEADS)
V_DIM = 2 * HEAD_DIM
ATTN_WIDTH = N_HEADS * V_DIM
QK_WIDTH = 2 * N_HEADS * HEAD_DIM
Q_BLOCK = 128
NORM_EPS = 1e-6
SPLITS = (CONV_WIDTH, CONV_WIDTH, CONV_WIDTH, QK_WIDTH, QK_WIDTH, ATTN_WIDTH, ATTN_WIDTH, D_MODEL, D_MODEL)
IN_WIDTH = 3 * CONV_WIDTH + 2 * QK_WIDTH + 2 * ATTN_WIDTH + 2 * D_MODEL

kernel_name = "hybrid_conformer_conv_diff_attn_gated_merge"


def rms_norm(x, g):
    xf = x.astype(jnp.float32)
    y = xf * lax.rsqrt(jnp.mean(xf * xf, axis=-1, keepdims=True) + NORM_EPS)
    return (y * g.astype(jnp.float32)).astype(x.dtype)


def layer_norm(x, g, b):
    xf = x.astype(jnp.float32)
    mu = jnp.mean(xf, axis=-1, keepdims=True)
    var = jnp.mean(jnp.square(xf - mu), axis=-1, keepdims=True)
    y = (xf - mu) * lax.rsqrt(var + NORM_EPS)
    return (y * g.astype(jnp.float32) + b.astype(jnp.float32)).astype(x.dtype)


def alibi_slopes():
    return 2.0 ** (-8.0 * jnp.arange(1, N_HEADS + 1, dtype=jnp.float32) / N_HEADS)


def lambda_init(layer_idx):
    return 0.8 - 0.6 * math.exp(-0.3 * layer_idx)


def conv_branch(cv, cg, cz, conv_w, conv_b, cn_g, cn_b, w_conv_out):
    u = cv * jax.nn.sigmoid(cg)
    k = conv_w.reshape(CONV_KERNEL, 1, CONV_WIDTH).astype(u.dtype)
    u = lax.conv_general_dilated(u, k, window_strides=(1,), padding=[(CONV_KERNEL - 1, 0)],
                                 dimension_numbers=("NWC", "WIO", "NWC"),
                                 feature_group_count=CONV_WIDTH)
    u = u + conv_b
    u = layer_norm(u, cn_g, cn_b)
    u = jax.nn.silu(u) * jax.nn.silu(cz)
    return u @ w_conv_out


def diff_attention(q1, q2, k1, k2, v, lam):
    B, H, S, _ = q1.shape
    nblk = S // Q_BLOCK
    scale = HEAD_DIM ** -0.5
    slopes = alibi_slopes()
    kpos = jnp.arange(S)
    qb1 = q1.reshape(B, H, nblk, Q_BLOCK, HEAD_DIM).transpose(2, 0, 1, 3, 4)
    qb2 = q2.reshape(B, H, nblk, Q_BLOCK, HEAD_DIM).transpose(2, 0, 1, 3, 4)

    def one_block(args):
        i, a1, a2 = args
        qpos = i * Q_BLOCK + jnp.arange(Q_BLOCK)
        dist = (qpos[:, None] - kpos[None, :]).astype(jnp.float32)
        bias = jnp.where(dist[None] >= 0, -slopes[:, None, None] * dist[None], -jnp.inf)
        s1 = jnp.einsum("bhqd,bhkd->bhqk", a1, k1).astype(jnp.float32) * scale + bias
        s2 = jnp.einsum("bhqd,bhkd->bhqk", a2, k2).astype(jnp.float32) * scale + bias
        attn = jax.nn.softmax(s1, axis=-1) - lam * jax.nn.softmax(s2, axis=-1)
        return jnp.einsum("bhqk,bhke->bhqe", attn.astype(v.dtype), v)

    out = lax.map(one_block, (jnp.arange(nblk), qb1, qb2))
    return out.transpose(1, 0, 3, 2, 4).reshape(B, S, H, V_DIM)


def attn_branch(q, k, v, az, layer_idx, lam_q1, lam_k1, lam_q2, lam_k2, subln_g, w_attn_out):
    B, S, _ = q.shape
    q = q.reshape(B, S, N_HEADS, 2, HEAD_DIM).transpose(0, 2, 3, 1, 4)
    k = k.reshape(B, S, N_HEADS, 2, HEAD_DIM).transpose(0, 2, 3, 1, 4)
    v = v.reshape(B, S, N_HEADS, V_DIM).transpose(0, 2, 1, 3)
    lam_0 = lambda_init(layer_idx)
    lam = (jnp.exp(jnp.sum(lam_q1.astype(jnp.float32) * lam_k1.astype(jnp.float32)))
           - jnp.exp(jnp.sum(lam_q2.astype(jnp.float32) * lam_k2.astype(jnp.float32))) + lam_0)
    o = diff_attention(q[:, :, 0], q[:, :, 1], k[:, :, 0], k[:, :, 1], v, lam)
    o = rms_norm(o, subln_g) * (1.0 - lam_0)
    o = o.reshape(B, S, ATTN_WIDTH) * jax.nn.silu(az)
    return o @ w_attn_out


def setup_inputs(seed: int = 0) -> dict:
    key = jax.random.key(seed)
    ks = jax.random.split(key, 24)
    f32 = jnp.float32
    nrm = lambda k, shape, s: jax.random.normal(k, shape, f32) * s
    return {
        "x": nrm(ks[0], (BATCH, SEQ, D_MODEL), 1.0),
        "c": nrm(ks[1], (BATCH, D_MODEL), 1.0),
        "w_ada": nrm(ks[2], (DEPTH, D_MODEL, 3 * D_MODEL), D_MODEL ** -0.5),
        "b_ada": nrm(ks[3], (DEPTH, 3 * D_MODEL), 0.01),
        "g_pre": 1.0 + nrm(ks[4], (DEPTH, D_MODEL), 0.02),
        "g_post": 1.0 + nrm(ks[5], (DEPTH, D_MODEL), 0.02),
        "w_in": nrm(ks[6], (DEPTH, D_MODEL, IN_WIDTH), D_MODEL ** -0.5),
        "conv_w": nrm(ks[7], (DEPTH, CONV_KERNEL, CONV_WIDTH), CONV_KERNEL ** -0.5),
        "conv_b": nrm(ks[8], (DEPTH, CONV_WIDTH), 0.01),
        "cn_g": 1.0 + nrm(ks[9], (DEPTH, CONV_WIDTH), 0.02),
        "cn_b": nrm(ks[10], (DEPTH, CONV_WIDTH), 0.01),
        "w_conv_out": nrm(ks[11], (DEPTH, CONV_WIDTH, D_MODEL), CONV_WIDTH ** -0.5),
        "lam_q1": nrm(ks[12], (DEPTH, HEAD_DIM), 0.1),
        "lam_k1": nrm(ks[13], (DEPTH, HEAD_DIM), 0.1),
        "lam_q2": nrm(ks[14], (DEPTH, HEAD_DIM), 0.1),
        "lam_k2": nrm(ks[15], (DEPTH, HEAD_DIM), 0.1),
        "subln_g": 1.0 + nrm(ks[16], (DEPTH, V_DIM), 0.02),
        "w_attn_out": nrm(ks[17], (DEPTH, ATTN_WIDTH, D_MODEL), ATTN_WIDTH ** -0.5),
        "w_o": nrm(ks[18], (DEPTH, D_MODEL, D_MODEL), D_MODEL ** -0.5),
    }


def reference(x, c, w_ada, b_ada, g_pre, g_post, w_in, conv_w, conv_b, cn_g, cn_b, w_conv_out,
              lam_q1, lam_k1, lam_q2, lam_k2, subln_g, w_attn_out, w_o):
    offsets = np.cumsum(np.array(SPLITS))[:-1].tolist()
    c_act = jax.nn.silu(c)
    for l in range(DEPTH):
        mod = c_act @ w_ada[l] + b_ada[l]
        shift, scale, gate = jnp.split(mod, 3, axis=-1)
        h = rms_norm(x, g_pre[l]) * (1.0 + scale[:, None, :]) + shift[:, None, :]
        p = h @ w_in[l]
        cv, cg, cz, q, k, v, az, m_conv, m_attn = jnp.split(p, offsets, axis=-1)
        y_conv = conv_branch(cv, cg, cz, conv_w[l], conv_b[l], cn_g[l], cn_b[l], w_conv_out[l])
        y_attn = attn_branch(q, k, v, az, l, lam_q1[l], lam_k1[l], lam_q2[l], lam_k2[l],
                             subln_g[l], w_attn_out[l])
        merged = jax.nn.sigmoid(m_conv) * y_conv + jax.nn.sigmoid(m_attn) * y_attn
        out = merged @ w_o[l]
        x = x + gate[:, None, :] * rms_norm(out, g_post[l])
    return x
```

```python
import math
import numpy as np
import concourse.bass as bass
import concourse.mybir as mybir
from concourse.bass_utils import run_bass_kernel_spmd

F32 = mybir.dt.float32
BF16 = mybir.dt.bfloat16
ALU = mybir.AluOpType
AF = mybir.ActivationFunctionType

D = 2048
CW = 1024
KCONV = 31
NH = 8
HD = 64
VD = 128
INW = 11264
EPS = 1e-6
TT = 512
NBLK_IN = INW // 512
DEPTH = 4
BATCH = 4
SEQ = 8192
NREL = 80


def lambda_init(layer_idx):
    return 0.8 - 0.6 * math.exp(-0.3 * layer_idx)


class _Op:
    __slots__ = ("eng", "emit", "deps", "sig", "sidx", "chan", "waits")


class Prog:
    def __init__(self, nc):
        self.nc = nc
        self.ops = []
        self.lastw = {}
        self.readers = {}
        self.frozen = set()

    def freeze(self, *keys):
        for k in keys:
            self.frozen.add(k)

    def add(self, eng, emit, reads=(), writes=(), chan=None):
        op = _Op()
        op.eng = eng
        op.emit = emit
        op.chan = chan
        op.sig = chan is not None
        op.sidx = 0
        op.waits = None
        idx = len(self.ops)
        stream = chan if chan is not None else eng
        deps = set()
        for r in reads:
            w = self.lastw.get(r)
            if w is not None:
                deps.add(w)
        for k in writes:
            w = self.lastw.get(k)
            if w is not None:
                deps.add(w)
            rd = self.readers.get(k)
            if rd:
                deps.update(rd.values())
        for r in reads:
            if r in self.frozen:
                continue
            self.readers.setdefault(r, {})[stream] = idx
        for k in writes:
            self.lastw[k] = idx
            self.readers[k] = {}
        deps.discard(idx)
        op.deps = deps
        self.ops.append(op)
        return idx

    def finalize(self):
        ops = self.ops
        for op in ops:
            for d in op.deps:
                dop = ops[d]
                if dop.chan is None and dop.eng == "pe" and op.eng == "pe" and op.chan is None:
                    continue
                dop.sig = True
        cnt = {}
        for op in ops:
            if op.sig:
                s = op.chan if op.chan is not None else op.eng
                cnt[s] = cnt.get(s, 0) + 1
                op.sidx = cnt[s]
        waited = {}
        for op in ops:
            wd = waited.setdefault(op.eng, {})
            need = {}
            for d in op.deps:
                dop = ops[d]
                if not dop.sig:
                    continue
                if dop.chan is None and dop.eng == "pe" and op.eng == "pe" and op.chan is None:
                    continue
                if dop.chan is not None:
                    s, v = dop.chan, 16 * dop.sidx
                else:
                    s, v = dop.eng, dop.sidx
                if wd.get(s, 0) >= v:
                    continue
                if need.get(s, 0) < v:
                    need[s] = v
            for s, v in need.items():
                wd[s] = v
            op.waits = list(need.items())
        self.streams = sorted(cnt.keys())
        return cnt

    def emit_all(self, block, sems, final_waits):
        nc = self.nc
        ops = self.ops
        engs = {"pe": block.tensor, "act": block.scalar, "dve": block.vector, "pool": block.gpsimd,
                "sp": block.sync}
        for ename, deco in engs.items():
            mine = [op for op in ops if op.eng == ename]

            def body(e, mine=mine, ename=ename):
                for op in mine:
                    for s, v in op.waits:
                        e.wait_ge(sems[s], v)
                    ins = op.emit(e)
                    if op.sig:
                        if op.chan is not None:
                            ins.then_inc(sems[op.chan], 16)
                        else:
                            ins.then_inc(sems[ename], 1)
                if ename == "sp":
                    for s, v in final_waits:
                        e.wait_ge(sems[s], v)

            deco(body)


def build_program(T, L, last_to_out=True):
    NT = T // TT
    NKT = T // 128
    nc = bass.Bass("TRN2", target_bir_lowering=False)
    dt = nc.dram_tensor

    x_in = dt("x", [T, D], F32, kind="ExternalInput").ap()
    cT_in = dt("cT", [128, 16], F32, kind="ExternalInput").ap()
    w_ada_in = dt("w_ada", [L, D, 3 * D], F32, kind="ExternalInput").ap()
    b_ada_in = dt("b_ada_t", [128, L * 48], F32, kind="ExternalInput").ap()
    g_pre_in = dt("g_pre_t", [128, L * 16], F32, kind="ExternalInput").ap()
    g_post_in = dt("g_post_t", [128, L * 16], F32, kind="ExternalInput").ap()
    w_in_in = dt("w_in", [L, D, INW], F32, kind="ExternalInput").ap()
    conv_w_in = dt("conv_w_t", [128, L * 8 * KCONV], F32, kind="ExternalInput").ap()
    conv_b_in = dt("conv_b_t", [128, L * 8], F32, kind="ExternalInput").ap()
    cn_g_in = dt("cn_g_t", [128, L * 8], F32, kind="ExternalInput").ap()
    cn_b_in = dt("cn_b_t", [128, L * 8], F32, kind="ExternalInput").ap()
    w_co_in = dt("w_conv_out", [L, CW, D], F32, kind="ExternalInput").ap()
    lam_in = dt("lam_t", [128, L * 4 * HD], F32, kind="ExternalInput").ap()
    subln_in = dt("subln_t", [128, L], F32, kind="ExternalInput").ap()
    w_ao_in = dt("w_attn_out", [L, CW, D], F32, kind="ExternalInput").ap()
    w_o_in = dt("w_o", [L, D, D], F32, kind="ExternalInput").ap()
    ident_in = dt("ident", [128, 128], F32, kind="ExternalInput").ap()
    trimask_in = dt("trimask", [128, 128], F32, kind="ExternalInput").ap()
    kaug_in = dt("kaug", [128, 2048], F32, kind="ExternalInput").ap()
    qaug_in = dt("qaug", [128, NH * 512], F32, kind="ExternalInput").ap()
    btab_in = dt("btab", [128, NH * NREL], F32, kind="ExternalInput").ap()

    out_d = dt("out", [T, D], F32, kind="ExternalOutput").ap()

    xT_d = dt("xT_s", [D, T], F32, kind="Internal").ap()
    wi_d = dt("wi_s", [L * NBLK_IN * 128, 16 * 512], BF16, kind="Internal").ap()
    wco_d = dt("wco_s", [L * 4 * 128, 8 * 512], BF16, kind="Internal").ap()
    wao_d = dt("wao_s", [L * 4 * 128, 8 * 512], BF16, kind="Internal").ap()
    wo_d = dt("wo_s", [L * 4 * 128, 16 * 512], BF16, kind="Internal").ap()
    KT_d = dt("KT_s", [NH * 128, T], BF16, kind="Internal").ap()
    Vh_d = dt("Vh_s", [NH * 128, NKT * 128], BF16, kind="Internal").ap()

    P = Prog(nc)
    sb = nc.alloc_sbuf_tensor
    A32 = sb("A32", [128, 16 * 512], F32).ap()
    U32 = sb("U32", [128, 8 * 544], F32).ap()
    XIN = sb("XIN", [128, 2 * 512], F32).ap()
    HT = sb("HT", [128, 16 * 512], BF16).ap()
    WB = sb("WB", [128, 2 * 8192], BF16).ap()
    SCZ = sb("SCZ", [128, 8 * 512], BF16).ap()
    QT = sb("QT", [128, 16 * 512], BF16).ap()
    AZ = sb("AZ", [128, 8 * 512], BF16).ap()
    MK = sb("MK", [128, 16 * 512], BF16).ap()
    VB = sb("VB", [128, 2 * 2048], BF16).ap()
    EB = sb("EB", [128, 3 * 1024], BF16).ap()
    TF = sb("TF", [128, 6 * 512], F32).ap()
    TB = sb("TB", [128, 4 * 512], BF16).ap()
    STG = sb("STG", [128, 6 * 512], BF16).ap()
    HALO = sb("HALO", [128, 8 * 32], F32).ap()
    ONES = sb("ONES", [128, 128], BF16).ap()
    IDENT = sb("IDENT", [128, 128], F32).ap()
    TRI = sb("TRI", [128, 128], BF16).ap()
    KAUG = sb("KAUGC", [128, 2048], BF16).ap()
    BTAB = sb("BTAB", [128, NH * NREL], F32).ap()
    CACT = sb("CACT", [128, 16], F32).ap()
    MOD = sb("MOD", [128, L * 48], F32).ap()
    GM = sb("GM", [128, L * 16], F32).ap()
    GG = sb("GG", [128, L * 16], F32).ap()
    GPRE = sb("GPRE", [128, L * 16], F32).ap()
    GPOST = sb("GPOST", [128, L * 16], F32).ap()
    BADA = sb("BADA", [128, L * 48], F32).ap()
    CONVW = sb("CONVW", [128, L * 8 * KCONV], F32).ap()
    CONVB = sb("CONVB", [128, L * 8], F32).ap()
    CNG = sb("CNG", [128, L * 8], F32).ap()
    CNB = sb("CNB", [128, L * 8], F32).ap()
    LAMV = sb("LAMV", [128, L * 4 * HD], F32).ap()
    LAMT = sb("LAMT", [128, 2 * HD], F32).ap()
    LAMS = sb("LAMS", [128, 8], F32).ap()
    NLAM = sb("NLAM", [128, L], F32).ap()
    GSUB = sb("GSUB", [128, L], F32).ap()
    EPSC_T = sb("EPSC", [128, 1], F32).ap()
    EPSC = EPSC_T[:, 0:1]
    PS = nc.alloc_psum_tensor("PS", [128, 8 * 512], F32).ap()

    def ps(bank, n=512, off=0):
        return PS[:, bank * 512 + off: bank * 512 + off + n]

    def tf(i):
        return TF[:, i * 512:(i + 1) * 512]

    def tb(i):
        return TB[:, i * 512:(i + 1) * 512]

    ctr = {}

    def rr(name, n):
        v = ctr.get(name, 0)
        ctr[name] = v + 1
        return v % n

    def dma(q, out, in_, reads, writes, chan):
        P.add(q, lambda e, out=out, in_=in_: e.dma_start(out=out, in_=in_), reads, writes, chan=chan)

    def act(out, in_, func, reads, writes, bias=None, scale=None):
        kw = {}
        if bias is not None:
            kw["bias"] = bias
        if scale is not None:
            kw["scale"] = scale
        P.add("act", lambda e, out=out, in_=in_, func=func, kw=kw: e.activation(out, in_, func, **kw),
              reads, writes)

    def tt(out, in0, in1, op, reads, writes, eng="dve"):
        P.add(eng, lambda e, out=out, in0=in0, in1=in1, op=op: e.tensor_tensor(out, in0, in1, op),
              reads, writes)

    def ts(out, in0, s1, s2, op0, op1, reads, writes, eng="dve"):
        if op1 is None:
            P.add(eng, lambda e, out=out, in0=in0, s1=s1, op0=op0: e.tensor_scalar(out, in0, s1, None, op0),
                  reads, writes)
        else:
            P.add(eng, lambda e, out=out, in0=in0, s1=s1, s2=s2, op0=op0, op1=op1:
                  e.tensor_scalar(out, in0, s1, s2, op0, op1), reads, writes)

    def recip(out, in_, reads, writes):
        P.add("dve", lambda e, out=out, in_=in_: e.reciprocal(out, in_), reads, writes)

    def stt(out, in0, sc, in1, op0, op1, reads, writes, eng="dve"):
        P.add(eng, lambda e, out=out, in0=in0, sc=sc, in1=in1, op0=op0, op1=op1:
              e.scalar_tensor_tensor(out, in0, sc, in1, op0, op1), reads, writes)

    def mm(out, lhsT, rhs, start, stop, reads, writes):
        P.add("pe", lambda e, out=out, lhsT=lhsT, rhs=rhs, start=start, stop=stop:
              e.matmul(out, lhsT, rhs, start=start, stop=stop), reads, writes)

    def tr(out, in_, reads, writes):
        P.add("pe", lambda e, out=out, in_=in_: e.transpose(out, in_, IDENT), reads, writes)

    dma("sp", IDENT, ident_in, (), (("IDENT",),), "c0")
    dma("sp", BTAB, btab_in, (), (("BTAB",),), "c1")
    dma("sp", tf(0)[:, 0:128], trimask_in, (), (("TF", 0),), "c2")
    act(TRI, tf(0)[:, 0:128], AF.Copy, (("TF", 0),), (("TRI",),))
    P.add("dve", lambda e: e.memset(ONES, 1.0), (), (("ONES",),))
    P.add("dve", lambda e: e.memset(EPSC_T, EPS), (), (("EPSC",),))
    P.add("dve", lambda e: e.memset(HALO, 0.0), (), [("HALO", c) for c in range(8)])
    dma("sp", A32[:, 0:2048], kaug_in, (), [("A32", k) for k in range(4)], "A32")
    act(KAUG[64:67, :], A32[64:67, 0:2048], AF.Copy, [("A32", k) for k in range(4)], (("KAUG",),))
    dma("sp", A32[:, 4096:8192], qaug_in, (), [("A32", k) for k in range(8, 16)], "c13")
    for h in range(NH):
        for m in range(2):
            act(QT[64:67, (2 * h + m) * 512:(2 * h + m + 1) * 512], A32[64:67, 4096 + h * 512:4096 + (h + 1) * 512],
                AF.Copy, [("A32", k) for k in range(8, 16)], (("QTaug", h, m),))
    for (dst, src, key, ch) in ((GPRE, g_pre_in, "GPRE", "c3"), (GPOST, g_post_in, "GPOST", "c4"),
                                (BADA, b_ada_in, "BADA", "c5"), (CONVW, conv_w_in, "CONVW", "c6"),
                                (CONVB, conv_b_in, "CONVB", "c7"), (CNG, cn_g_in, "CNG", "c8"),
                                (CNB, cn_b_in, "CNB", "c9"), (LAMV, lam_in, "LAMV", "c10"),
                                (GSUB, subln_in, "GSUBraw", "c11"), (CACT, cT_in, "CACTraw", "c12")):
        dma("sp", dst, src, (), ((key,),), ch)
    act(CACT, CACT, AF.Silu, (("CACTraw",),), (("CACT",),))
    for l in range(L):
        lam0 = lambda_init(l)
        b0 = l * 4 * HD
        tt(LAMT[:, 0:HD], LAMV[:, b0:b0 + HD], LAMV[:, b0 + HD:b0 + 2 * HD], ALU.mult, (("LAMV",),), (("LAMT",),))
        tt(LAMT[:, HD:2 * HD], LAMV[:, b0 + 2 * HD:b0 + 3 * HD], LAMV[:, b0 + 3 * HD:b0 + 4 * HD], ALU.mult,
           (("LAMV",), ("LAMT",)), (("LAMT",),))
        P.add("dve", lambda e: e.reduce_sum(LAMS[:, 0:1], LAMT[:, 0:HD], mybir.AxisListType.X),
              (("LAMT",),), (("LAMS",),))
        P.add("dve", lambda e: e.reduce_sum(LAMS[:, 1:2], LAMT[:, HD:2 * HD], mybir.AxisListType.X),
              (("LAMT",), ("LAMS",)), (("LAMS",),))
        act(LAMS[:, 2:4], LAMS[:, 0:2], AF.Exp, (("LAMS",),), (("LAMS",),))
        stt(NLAM[:, l:l + 1], LAMS[:, 3:4], -lam0, LAMS[:, 2:3], ALU.add, ALU.subtract,
            (("LAMS",),), (("NLAM", l),))
        ts(GSUB[:, l:l + 1], GSUB[:, l:l + 1], 1.0 - lam0, None, ALU.mult, None,
           (("GSUBraw",), ("GSUB", l - 1)), (("GSUB", l),))
    P.freeze(("EPSC",), ("IDENT",), ("BTAB",), ("TRI",), ("ONES",), ("KAUG",), ("CACT",), ("GPRE",), ("GPOST",), ("BADA",),
             ("CONVW",), ("CONVB",), ("CNG",), ("CNB",))

    for l in range(L):
        for grp in range(12):
            src = w_ada_in[l].rearrange("(kc p) n -> p kc n", p=128)[:, :, grp * 512:(grp + 1) * 512]
            dma("sp", A32.rearrange("p (kc n) -> p kc n", kc=16), src, (), [("A32", k) for k in range(16)], "A32")
            for s in range(4):
                oc = grp * 4 + s
                bank = rr("ps", 8)
                for kc in range(16):
                    mm(ps(bank, 1), A32[:, kc * 512 + s * 128: kc * 512 + (s + 1) * 128], CACT[:, kc:kc + 1],
                       kc == 0, kc == 15, [("A32", k) for k in range(16)] + [("CACT",)], (("ps", bank),))
                tt(MOD[:, l * 48 + oc: l * 48 + oc + 1], ps(bank, 1), BADA[:, l * 48 + oc: l * 48 + oc + 1], ALU.add,
                   (("ps", bank), ("BADA",)), (("MOD", l),))
        stt(GM[:, l * 16:(l + 1) * 16], MOD[:, l * 48 + 16:l * 48 + 32], 1.0, GPRE[:, l * 16:(l + 1) * 16],
            ALU.add, ALU.mult, (("MOD", l), ("GPRE",)), (("GM", l),))
        tt(GG[:, l * 16:(l + 1) * 16], MOD[:, l * 48 + 32:l * 48 + 48], GPOST[:, l * 16:(l + 1) * 16], ALU.mult,
           (("MOD", l), ("GPOST",)), (("GG", l),))

    cvn = [0]

    def convert(src_rows, dst_view, ncols):
        i = cvn[0]
        cvn[0] += 1
        s32 = i % 2
        sbf = i % 2
        stage = A32[:, s32 * 2048: s32 * 2048 + ncols]
        stb = HT[:, sbf * 2048: sbf * 2048 + ncols]
        dma("sp", stage, src_rows, (), (("A32", "cv", s32),), "cvl%d" % s32)
        if i % 2 == 0:
            act(stb, stage, AF.Copy, (("A32", "cv", s32),), (("HT", "cv", sbf),))
        else:
            P.add("dve", lambda e, stb=stb, stage=stage: e.tensor_copy(stb, stage),
                  (("A32", "cv", s32),), (("HT", "cv", sbf),))
        dma("pool", dst_view, stb.rearrange("p (b c) -> p b c", c=512), (("HT", "cv", sbf),),
            (("WSCRP", sbf),), "cvs%d" % sbf)

    P.add("dve", lambda e: e.memset(TF[:, 0:1], 0.0), (),
          [("A32", k) for k in range(16)] + [("A32", "cv", 0), ("A32", "cv", 1), ("TF", 0)])
    for l in range(L):
        wv = wi_d[l * NBLK_IN * 128:(l + 1) * NBLK_IN * 128, :].rearrange("(b p) c -> p b c", p=128)
        for kc in range(16):
            for pc in range(NBLK_IN // 2):
                convert(w_in_in[l, kc * 128:(kc + 1) * 128, pc * 1024:(pc + 1) * 1024],
                        wv[:, 2 * pc:2 * pc + 2, kc * 512:(kc + 1) * 512], 1024)
        for (src, dst, nkc) in ((w_co_in, wco_d, 8), (w_ao_in, wao_d, 8), (w_o_in, wo_d, 16)):
            wv2 = dst[l * 4 * 128:(l + 1) * 4 * 128, :].rearrange("(b p) c -> p b c", p=128)
            for kc in range(nkc):
                for pc in range(2):
                    convert(src[l, kc * 128:(kc + 1) * 128, pc * 1024:(pc + 1) * 1024],
                            wv2[:, 2 * pc:2 * pc + 2, kc * 512:(kc + 1) * 512], 1024)
    P.add("dve", lambda e: e.memset(TF[:, 0:1], 0.0), (),
          [("A32", "cv", 0), ("A32", "cv", 1), ("HT", "cv", 0), ("HT", "cv", 1), ("TF", 0)]
          + [("A32", k) for k in range(16)] + [("HT", k) for k in range(16)])

    def load_wblock(src_rows, nkc):
        slot = rr("wb", 2)
        view = WB[:, slot * 8192: slot * 8192 + nkc * 512]
        dma("sp", view, src_rows[:, 0:nkc * 512], (("WSCRP", 0), ("WSCRP", 1)), (("WB", slot),), "wb%d" % slot)
        return slot

    def wblk(slot, kc, s):
        return WB[:, slot * 8192 + kc * 512 + s * 128: slot * 8192 + kc * 512 + (s + 1) * 128]

    for l in range(L):
        lastl = (l == L - 1)
        for i in range(NT):
            t0 = i * TT
            if l == 0:
                for g in range(4):
                    xs = rr("xtok", 2)
                    xtok = U32[:, xs * 2176: xs * 2176 + 2048]
                    dma("sp", xtok, x_in[t0 + g * 128: t0 + (g + 1) * 128, :], (),
                        [("U32", c) for c in range(xs * 4, xs * 4 + 4)], "xtok%d" % xs)
                    for kq in range(4):
                        bank = rr("ps", 8)
                        for j in range(4):
                            kc = kq * 4 + j
                            tr(ps(bank, 128, j * 128), xtok[:, kc * 128:(kc + 1) * 128],
                               [("U32", c) for c in range(xs * 4, xs * 4 + 4)] + [("IDENT",)], (("ps", bank),))
                        dst = A32.rearrange("p (kc t) -> p kc t", kc=16)[:, kq * 4:kq * 4 + 4, g * 128:(g + 1) * 128]
                        srcp = ps(bank).rearrange("p (j t) -> p j t", j=4)
                        if (g + kq) % 2 == 0:
                            act(dst, srcp, AF.Copy, (("ps", bank),), [("A32", kq * 4 + j) for j in range(4)])
                        else:
                            P.add("dve", lambda e, dst=dst, srcp=srcp: e.tensor_copy(dst, srcp), (("ps", bank),),
                                  [("A32", kq * 4 + j) for j in range(4)])
                dma("pool", xT_d.rearrange("(kc p) t -> p kc t", p=128)[:, :, t0:t0 + TT],
                    A32.rearrange("p (kc t) -> p kc t", kc=16), [("A32", k) for k in range(16)],
                    (("xT", i),), "A32st")
            else:
                dma("sp", A32.rearrange("p (kc t) -> p kc t", kc=16),
                    xT_d.rearrange("(kc p) t -> p kc t", p=128)[:, :, t0:t0 + TT], (("xT", i),),
                    [("A32", k) for k in range(16)], "A32")
            bank = rr("ps", 8)
            for kc in range(16):
                tbi = rr("tb", 4)
                act(tb(tbi), A32[:, kc * 512:(kc + 1) * 512], AF.Square, (("A32", kc),), (("TB", tbi),))
                mm(ps(bank), ONES, tb(tbi), kc == 0, kc == 15, (("TB", tbi), ("ONES",)), (("ps", bank),))
            act(tf(0), ps(bank), AF.Sqrt, (("ps", bank), ("EPSC",)), (("TF", 0),), bias=EPSC, scale=1.0 / D)
            recip(tf(0), tf(0), (("TF", 0),), (("TF", 0),))
            for kc in range(16):
                tfi = 1 + rr("tfA", 2)
                tt(tf(tfi), A32[:, kc * 512:(kc + 1) * 512], tf(0), ALU.mult, (("A32", kc), ("TF", 0)), (("TF", tfi),))
                act(HT[:, kc * 512:(kc + 1) * 512], tf(tfi), AF.Identity, (("TF", tfi), ("GM", l), ("MOD", l)),
                    (("HT", kc),), bias=MOD[:, l * 48 + kc: l * 48 + kc + 1], scale=GM[:, l * 16 + kc: l * 16 + kc + 1])
            HTall = [("HT", k) for k in range(16)]

            def proj_fm(blk, s, slot, bank):
                for kc in range(16):
                    mm(ps(bank), wblk(slot, kc, s), HT[:, kc * 512:(kc + 1) * 512], kc == 0, kc == 15,
                       (("WB", slot), ("HT", kc)), (("ps", bank),))

            def wi_rows(blk):
                r0 = (l * NBLK_IN + blk) * 128
                return wi_d[r0:r0 + 128, :]

            for blk in range(14):
                slot = load_wblock(wi_rows(blk), 16)
                if blk in (10, 11):
                    for g in range(4):
                        bank = rr("ps", 8)
                        for kc in range(16):
                            mm(ps(bank), HT[:, kc * 512 + g * 128: kc * 512 + (g + 1) * 128],
                               WB[:, slot * 8192 + kc * 512: slot * 8192 + (kc + 1) * 512], kc == 0, kc == 15,
                               (("WB", slot), ("HT", kc)), (("ps", bank),))
                        st = rr("stg", 6)
                        stg = STG[:, st * 512:(st + 1) * 512]
                        if g % 2 == 0:
                            act(stg, ps(bank), AF.Copy, (("ps", bank),), (("STG", st),))
                        else:
                            P.add("dve", lambda e, stg=stg, bank=bank: e.tensor_copy(stg, ps(bank)), (("ps", bank),),
                                  (("STG", st),))
                        kt = i * 4 + g
                        h0 = (blk - 10) * 4
                        dstv = Vh_d.rearrange("(h p) (n d) -> p h n d", p=128, d=128)[:, h0:h0 + 4, kt, :]
                        dma("pool", dstv, stg.rearrange("p (h d) -> p h d", h=4), (("STG", st),),
                            [("Vh", h0 + hh, i) for hh in range(4)], "stg%d" % st)
                    continue
                for s in range(4):
                    bank = rr("ps", 8)
                    proj_fm(blk, s, slot, bank)
                    c = (blk % 2) * 4 + s
                    if blk in (0, 1):
                        act(U32[:, c * 544 + 30: c * 544 + 542], ps(bank), AF.Copy, (("ps", bank),), (("U32", c),))
                    elif blk in (2, 3):
                        tfi = 3 + rr("tfB", 2)
                        act(tf(tfi), ps(bank), AF.Sigmoid, (("ps", bank),), (("TF", tfi),))
                        tt(U32[:, c * 544 + 30: c * 544 + 542], U32[:, c * 544 + 30: c * 544 + 542], tf(tfi), ALU.mult,
                           (("U32", c), ("TF", tfi)), (("U32", c),))
                    elif blk in (4, 5):
                        act(SCZ[:, c * 512:(c + 1) * 512], ps(bank), AF.Silu, (("ps", bank),), (("SCZ", c),))
                    elif blk in (6, 7):
                        act(QT[0:64, (2 * c) * 512:(2 * c + 1) * 512], ps(bank)[0:64, :], AF.Identity, (("ps", bank),),
                            (("QT", c, 0),), scale=HD ** -0.5)
                        st = rr("stg", 6)
                        stg = STG[:, st * 512:(st + 1) * 512]
                        act(stg[64:128, :], ps(bank)[64:128, :], AF.Identity, (("ps", bank),), (("STG", st),),
                            scale=HD ** -0.5)
                        dma("pool", QT[0:64, (2 * c + 1) * 512:(2 * c + 2) * 512], stg[64:128, :], (("STG", st),),
                            (("QT", c, 1),), "stg%d" % st)
                    elif blk in (8, 9):
                        st = rr("stg", 6)
                        stg = STG[:, st * 512:(st + 1) * 512]
                        P.add("dve", lambda e, stg=stg, bank=bank: e.tensor_copy(stg, ps(bank)), (("ps", bank),),
                              (("STG", st),))
                        dma("pool", KT_d[c * 128:(c + 1) * 128, t0:t0 + TT], stg, (("STG", st),), (("KT", c, i),),
                            "stg%d" % st)
                    elif blk in (12, 13):
                        act(AZ[:, c * 512:(c + 1) * 512], ps(bank), AF.Silu, (("ps", bank),), (("AZ", c),))

            bank_m = rr("ps", 8)
            bank_q = rr("ps", 8)
            for c in range(8):
                ub = c * 544
                if i == 0:
                    P.add("dve", lambda e, ub=ub: e.memset(U32[:, ub:ub + 30], 0.0), (("U32", c),), (("U32", c),))
                else:
                    P.add("dve", lambda e, ub=ub, c=c: e.tensor_copy(U32[:, ub:ub + 30], HALO[:, c * 32:c * 32 + 30]),
                          (("HALO", c), ("U32", c)), (("U32", c),))
                P.add("dve", lambda e, ub=ub, c=c: e.tensor_copy(HALO[:, c * 32:c * 32 + 30], U32[:, ub + 512:ub + 542]),
                      (("U32", c),), (("HALO", c),))
                acc = tf(1 + rr("tfA", 2))
                acck = ("TF", 1 + ((ctr["tfA"] - 1) % 2))
                wb0 = (l * 8 + c) * KCONV
                ts(acc, U32[:, ub:ub + 512], CONVW[:, wb0:wb0 + 1], CONVB[:, l * 8 + c:l * 8 + c + 1], ALU.mult, ALU.add,
                   (("U32", c), ("CONVW",), ("CONVB",)), (acck,))
                for j in range(1, KCONV):
                    stt(acc, U32[:, ub + j:ub + j + 512], CONVW[:, wb0 + j:wb0 + j + 1], acc, ALU.mult, ALU.add,
                        (("U32", c), ("CONVW",), acck), (acck,))
                P.add("dve", lambda e, ub=ub, acc=acc: e.tensor_copy(U32[:, ub + 30:ub + 542], acc), (acck,),
                      (("U32", c),))
                t1 = rr("tb", 4)
                act(tb(t1), acc, AF.Copy, (acck,), (("TB", t1),))
                mm(ps(bank_m), ONES, tb(t1), c == 0, c == 7, (("TB", t1), ("ONES",)), (("ps", bank_m),))
                t2 = rr("tb", 4)
                act(tb(t2), acc, AF.Square, (acck,), (("TB", t2),))
                mm(ps(bank_q), ONES, tb(t2), c == 0, c == 7, (("TB", t2), ("ONES",)), (("ps", bank_q),))
            ts(tf(3), ps(bank_m), 1.0 / CW, None, ALU.mult, None, (("ps", bank_m),), (("TF", 3),))
            tt(tf(5), tf(3), tf(3), ALU.mult, (("TF", 3),), (("TF", 5),))
            stt(tf(4), ps(bank_q), 1.0 / CW, tf(5), ALU.mult, ALU.subtract, (("ps", bank_q), ("TF", 5)), (("TF", 4),))
            act(tf(4), tf(4), AF.Sqrt, (("TF", 4), ("EPSC",)), (("TF", 4),), bias=EPSC, scale=1.0)
            recip(tf(4), tf(4), (("TF", 4),), (("TF", 4),))
            for c in range(8):
                ub = c * 544
                tfi = 1 + rr("tfA", 2)
                tt(tf(tfi), U32[:, ub + 30:ub + 542], tf(3), ALU.subtract, (("U32", c), ("TF", 3)), (("TF", tfi),))
                tt(tf(tfi), tf(tfi), tf(4), ALU.mult, (("TF", tfi), ("TF", 4)), (("TF", tfi),))
                act(tf(tfi), tf(tfi), AF.Silu, (("TF", tfi), ("CNG",), ("CNB",)), (("TF", tfi),),
                    bias=CNB[:, l * 8 + c:l * 8 + c + 1], scale=CNG[:, l * 8 + c:l * 8 + c + 1])
                tt(SCZ[:, c * 512:(c + 1) * 512], tf(tfi), SCZ[:, c * 512:(c + 1) * 512], ALU.mult,
                   (("TF", tfi), ("SCZ", c)), (("SCZ", c),))

            for sl in range(2):
                for m in range(2):
                    seg = sl * 2 + m
                    act(MK[64:67, seg * 2048:(seg + 1) * 2048], KAUG[64:67, :], AF.Copy,
                        [("MK", seg * 4 + j) for j in range(4)] + [("KAUG",)], [("MK", seg * 4 + j) for j in range(4)])
            nkt = 4 * (i + 1)
            nspan = (nkt + 15) // 16
            for h in range(NH):
                for sp in range(nspan):
                    ntl = min(16, nkt - sp * 16)
                    sl = rr("kb", 2)
                    tiles = range(sp * 4, sp * 4 + (ntl + 3) // 4)
                    for m in range(2):
                        seg = sl * 2 + m
                        dma("sp", MK[0:64, seg * 2048: seg * 2048 + ntl * 128],
                            KT_d[h * 128 + m * 64: h * 128 + (m + 1) * 64, sp * 2048: sp * 2048 + ntl * 128],
                            [("KT", h, tq) for tq in tiles], [("MK", seg * 4 + j) for j in range(4)],
                            "kb%d%d" % (sl, m))
                    dma("sp", VB[:, sl * 2048: sl * 2048 + ntl * 128],
                        Vh_d[h * 128:(h + 1) * 128, sp * 2048: sp * 2048 + ntl * 128],
                        [("Vh", h, tq) for tq in tiles], (("VB", sl),), "vb%d" % sl)
                    for ktl in range(ntl):
                        kt = sp * 16 + ktl
                        mdiag = kt - 4 * i
                        c0 = 128 * max(mdiag, 0)
                        n = 512 - c0
                        rel = 4 * i - kt
                        sb2 = rr("sbank", 2)
                        for m in range(2):
                            seg = sl * 2 + m
                            mm(ps(sb2 * 2 + m, n, c0), MK[0:67, seg * 2048 + ktl * 128: seg * 2048 + (ktl + 1) * 128],
                               QT[0:67, (2 * h + m) * 512 + c0:(2 * h + m + 1) * 512], True, True,
                               [("MK", seg * 4 + j) for j in range(4)] + [("QT", h, m), ("QTaug", h, m)],
                               (("ps", sb2 * 2 + m),))
                        eb = rr("eb", 3)
                        ein = PS[:, sb2 * 1024:(sb2 + 1) * 1024].rearrange("p (m q) -> p m q", m=2)[:, :, c0:512]
                        eout = EB[:, eb * 1024:(eb + 1) * 1024].rearrange("p (m q) -> p m q", m=2)[:, :, c0:512]
                        act(eout, ein, AF.Exp, (("ps", sb2 * 2), ("ps", sb2 * 2 + 1), ("BTAB",)), (("EB", eb),),
                            bias=BTAB[:, h * NREL + rel + 8: h * NREL + rel + 9], scale=1.0)
                        if mdiag >= 0:
                            for m in range(2):
                                blkv = EB[:, eb * 1024 + m * 512 + c0: eb * 1024 + m * 512 + c0 + 128]
                                tt(blkv, blkv, TRI, ALU.mult, (("EB", eb), ("TRI",)), (("EB", eb),))
                        first = (kt == 0)
                        last = (kt == nkt - 1)
                        for m in range(2):
                            erhs = EB[:, eb * 1024 + m * 512 + c0: eb * 1024 + (m + 1) * 512]
                            mm(ps(4 + m, n, c0), VB[:, sl * 2048 + ktl * 128: sl * 2048 + (ktl + 1) * 128], erhs,
                               first, last, (("VB", sl), ("EB", eb)), (("ps", 4 + m),))
                            mm(ps(6 + m, n, c0), ONES, erhs, first, last, (("EB", eb), ("ONES",)), (("ps", 6 + m),))
                P.add("dve", lambda e: e.reciprocal(tf(1), ps(6)), (("ps", 6),), (("TF", 1),))
                P.add("dve", lambda e: e.reciprocal(tf(2), ps(7)), (("ps", 7),), (("TF", 2),))
                tt(tf(1), ps(4), tf(1), ALU.mult, (("ps", 4), ("TF", 1)), (("TF", 1),))
                tt(tf(2), ps(5), tf(2), ALU.mult, (("ps", 5), ("TF", 2)), (("TF", 2),))
                stt(tf(1), tf(2), NLAM[:, l:l + 1], tf(1), ALU.mult, ALU.add, (("TF", 1), ("TF", 2), ("NLAM", l)),
                    (("TF", 1),))
                t1 = rr("tb", 4)
                act(tb(t1), tf(1), AF.Square, (("TF", 1),), (("TB", t1),))
                mm(ps(6), ONES, tb(t1), True, True, (("TB", t1), ("ONES",)), (("ps", 6),))
                act(tf(2), ps(6), AF.Sqrt, (("ps", 6), ("EPSC",)), (("TF", 2),), bias=EPSC, scale=1.0 / VD)
                recip(tf(2), tf(2), (("TF", 2),), (("TF", 2),))
                tt(tf(1), tf(1), tf(2), ALU.mult, (("TF", 1), ("TF", 2)), (("TF", 1),))
                stt(AZ[:, h * 512:(h + 1) * 512], tf(1), GSUB[:, l:l + 1], AZ[:, h * 512:(h + 1) * 512], ALU.mult,
                    ALU.mult, (("TF", 1), ("GSUB", l), ("AZ", h)), (("AZ", h),))

            GC = U32[:, 0:2048]
            GA = U32[:, 2176:2176 + 2048]
            gck = [("U32", c) for c in range(4)]
            gak = [("U32", c) for c in range(4, 8)]
            for ob in range(4):
                slot = load_wblock(wi_rows(14 + ob), 16)
                for s in range(4):
                    bank = rr("ps", 8)
                    proj_fm(14 + ob, s, slot, bank)
                    act(GC[:, s * 512:(s + 1) * 512], ps(bank), AF.Sigmoid, (("ps", bank),), gck)
                slot = load_wblock(wi_rows(18 + ob), 16)
                for s in range(4):
                    bank = rr("ps", 8)
                    proj_fm(18 + ob, s, slot, bank)
                    act(GA[:, s * 512:(s + 1) * 512], ps(bank), AF.Sigmoid, (("ps", bank),), gak)
                r0 = (l * 4 + ob) * 128
                slot = load_wblock(wco_d[r0:r0 + 128, :], 8)
                for s in range(4):
                    bank = rr("ps", 8)
                    for kc in range(8):
                        mm(ps(bank), wblk(slot, kc, s), SCZ[:, kc * 512:(kc + 1) * 512], kc == 0, kc == 7,
                           (("WB", slot), ("SCZ", kc)), (("ps", bank),))
                    tt(GC[:, s * 512:(s + 1) * 512], ps(bank), GC[:, s * 512:(s + 1) * 512], ALU.mult,
                       [("ps", bank)] + gck, gck)
                slot = load_wblock(wao_d[r0:r0 + 128, :], 8)
                for s in range(4):
                    bank = rr("ps", 8)
                    for kc in range(8):
                        mm(ps(bank), wblk(slot, kc, s), AZ[:, kc * 512:(kc + 1) * 512], kc == 0, kc == 7,
                           (("WB", slot), ("AZ", kc)), (("ps", bank),))
                    tt(GA[:, s * 512:(s + 1) * 512], ps(bank), GA[:, s * 512:(s + 1) * 512], ALU.mult,
                       [("ps", bank)] + gak, gak)
                    oc = ob * 4 + s
                    tt(MK[:, oc * 512:(oc + 1) * 512], GC[:, s * 512:(s + 1) * 512], GA[:, s * 512:(s + 1) * 512],
                       ALU.add, gck + gak, (("MK", oc),))

            bank_o = rr("ps", 8)
            for ob in range(4):
                r0 = (l * 4 + ob) * 128
                slot = load_wblock(wo_d[r0:r0 + 128, :], 16)
                for s in range(4):
                    oc = ob * 4 + s
                    bank = rr("ps", 8)
                    if bank == bank_o:
                        bank = rr("ps", 8)
                    for kc in range(16):
                        mm(ps(bank), wblk(slot, kc, s), MK[:, kc * 512:(kc + 1) * 512], kc == 0, kc == 15,
                           (("WB", slot), ("MK", kc)), (("ps", bank),))
                    act(A32[:, oc * 512:(oc + 1) * 512], ps(bank), AF.Copy, (("ps", bank),), (("A32", oc),))
                    t1 = rr("tb", 4)
                    act(tb(t1), ps(bank), AF.Square, (("ps", bank),), (("TB", t1),))
                    mm(ps(bank_o), ONES, tb(t1), oc == 0, oc == 15, (("TB", t1), ("ONES",)), (("ps", bank_o),))
            act(tf(0), ps(bank_o), AF.Sqrt, (("ps", bank_o), ("EPSC",)), (("TF", 0),), bias=EPSC, scale=1.0 / D)
            recip(tf(0), tf(0), (("TF", 0),), (("TF", 0),))
            for oc in range(16):
                xs = rr("xin", 2)
                xin = XIN[:, xs * 512:(xs + 1) * 512]
                dma("sp", xin, xT_d[oc * 128:(oc + 1) * 128, t0:t0 + TT], (("xT", i),), (("XIN", xs),), "xin%d" % xs)
                tt(A32[:, oc * 512:(oc + 1) * 512], A32[:, oc * 512:(oc + 1) * 512], tf(0), ALU.mult,
                   (("A32", oc), ("TF", 0)), (("A32", oc),))
                stt(A32[:, oc * 512:(oc + 1) * 512], A32[:, oc * 512:(oc + 1) * 512], GG[:, l * 16 + oc:l * 16 + oc + 1],
                    xin, ALU.mult, ALU.add, (("A32", oc), ("GG", l), ("XIN", xs)), (("A32", oc),))
            if not lastl:
                dma("pool", xT_d.rearrange("(kc p) t -> p kc t", p=128)[:, :, t0:t0 + TT],
                    A32.rearrange("p (kc t) -> p kc t", kc=16), [("A32", k) for k in range(16)],
                    (("xT", i),), "A32st")
            else:
                for g in range(4):
                    osl = rr("ost", 2)
                    ost = U32[:, osl * 2176: osl * 2176 + 2048]
                    ostk = [("U32", c) for c in range(osl * 4, osl * 4 + 4)]
                    for kq in range(4):
                        bank = rr("ps", 8)
                        for j in range(4):
                            kc = kq * 4 + j
                            tr(ps(bank, 128, j * 128), A32[:, kc * 512 + g * 128: kc * 512 + (g + 1) * 128],
                               (("A32", kc), ("IDENT",)), (("ps", bank),))
                        if kq % 2 == 0:
                            act(ost[:, kq * 512:(kq + 1) * 512], ps(bank), AF.Copy, (("ps", bank),), ostk)
                        else:
                            P.add("dve", lambda e, ost=ost, kq=kq, bank=bank:
                                  e.tensor_copy(ost[:, kq * 512:(kq + 1) * 512], ps(bank)), (("ps", bank),), ostk)
                    dma("pool", out_d[t0 + g * 128: t0 + (g + 1) * 128, :], ost, ostk, (("OUT",),), "ost%d" % osl)

    cnt = P.finalize()
    sems = {s: nc.alloc_semaphore(name="s_" + s) for s in P.streams}
    final_waits = [(s, 16 * cnt[s]) for s in ("ost0", "ost1") if s in cnt]
    with nc.Block() as block:
        P.emit_all(block, sems, final_waits)
    return nc, len(P.ops)


_ALIBI = [2.0 ** (-8.0 * (h + 1) / NH) for h in range(NH)]


def host_constants():
    ident = np.eye(128, dtype=np.float32)
    kk = np.arange(128)
    trimask = (kk[:, None] <= kk[None, :]).astype(np.float32)
    kaug = np.zeros((128, 2048), np.float32)
    kaug[64, :] = 1.0
    kaug[65, :] = 1.0
    kaug[66, :] = np.arange(2048) % 128
    qaug = np.zeros((128, NH * 512), np.float32)
    ii = np.arange(512)
    for h in range(NH):
        s = _ALIBI[h]
        qaug[64, h * 512:(h + 1) * 512] = -s * (ii % 256)
        qaug[65, h * 512:(h + 1) * 512] = -s * 256.0 * (ii // 256)
        qaug[66, h * 512:(h + 1) * 512] = s
    btab = np.zeros((128, NH * NREL), np.float32)
    for h in range(NH):
        for r in range(NREL):
            btab[:, h * NREL + r] = -_ALIBI[h] * 128.0 * (r - 8)
    return {"ident": ident, "trimask": trimask, "kaug": kaug, "qaug": qaug, "btab": btab}


def host_layout(L, c_b, w_ada, b_ada, g_pre, g_post, w_in, conv_w, conv_b, cn_g, cn_b, w_conv_out,
                lam_q1, lam_k1, lam_q2, lam_k2, subln_g, w_attn_out, w_o):
    f = np.float32
    m = {}
    m["cT"] = np.ascontiguousarray(c_b.reshape(16, 128).T).astype(f)
    m["w_ada"] = np.ascontiguousarray(w_ada[:L])
    m["b_ada_t"] = np.ascontiguousarray(b_ada[:L].reshape(L, 48, 128).transpose(2, 0, 1).reshape(128, L * 48))
    m["g_pre_t"] = np.ascontiguousarray(g_pre[:L].reshape(L, 16, 128).transpose(2, 0, 1).reshape(128, L * 16))
    m["g_post_t"] = np.ascontiguousarray(g_post[:L].reshape(L, 16, 128).transpose(2, 0, 1).reshape(128, L * 16))
    m["w_in"] = np.ascontiguousarray(w_in[:L])
    m["conv_w_t"] = np.ascontiguousarray(
        conv_w[:L].reshape(L, KCONV, 8, 128).transpose(3, 0, 2, 1).reshape(128, L * 8 * KCONV))
    for nm, a in (("conv_b_t", conv_b), ("cn_g_t", cn_g), ("cn_b_t", cn_b)):
        m[nm] = np.ascontiguousarray(a[:L].reshape(L, 8, 128).transpose(2, 0, 1).reshape(128, L * 8))
    m["w_conv_out"] = np.ascontiguousarray(w_conv_out[:L])
    lam = np.stack([lam_q1[:L], lam_k1[:L], lam_q2[:L], lam_k2[:L]], axis=1).reshape(1, L * 4 * HD)
    m["lam_t"] = np.ascontiguousarray(np.broadcast_to(lam, (128, L * 4 * HD)))
    m["subln_t"] = np.ascontiguousarray(subln_g[:L].T)
    m["w_attn_out"] = np.ascontiguousarray(w_attn_out[:L])
    m["w_o"] = np.ascontiguousarray(w_o[:L])
    return {k: np.asarray(v, dtype=f) for k, v in m.items()}


def kernel(x, c, w_ada, b_ada, g_pre, g_post, w_in, conv_w, conv_b, cn_g, cn_b, w_conv_out,
           lam_q1, lam_k1, lam_q2, lam_k2, subln_g, w_attn_out, w_o):
    arrs = [np.asarray(a, dtype=np.float32) for a in
            (x, c, w_ada, b_ada, g_pre, g_post, w_in, conv_w, conv_b, cn_g, cn_b, w_conv_out,
             lam_q1, lam_k1, lam_q2, lam_k2, subln_g, w_attn_out, w_o)]
    x, c = arrs[0], arrs[1]
    B, S, _ = x.shape
    L = arrs[2].shape[0]
    nc, _ = build_program(S, L)
    consts = host_constants()
    in_maps = []
    for core in range(8):
        b = core % B
        mp = host_layout(L, c[b], *arrs[2:])
        mp.update(consts)
        mp["x"] = np.ascontiguousarray(x[b])
        in_maps.append(mp)
    res = run_bass_kernel_spmd(nc, in_maps, core_ids=list(range(8)))
    out = np.stack([np.asarray(res.results[b]["out"], dtype=np.float32) for b in range(B)], axis=0)
    return out
```

```python
import math
import numpy as np
import concourse.bass as bass
import concourse.mybir as mybir
from concourse.bass_utils import run_bass_kernel_spmd

F32 = mybir.dt.float32
BF16 = mybir.dt.bfloat16
ALU = mybir.AluOpType
AF = mybir.ActivationFunctionType

D = 2048
CW = 1024
KCONV = 31
NH = 8
HD = 64
VD = 128
INW = 11264
EPS = 1e-6
TT = 512
NBLK_IN = INW // 512
DEPTH = 4
BATCH = 4
SEQ = 8192
NREL = 80


def lambda_init(layer_idx):
    return 0.8 - 0.6 * math.exp(-0.3 * layer_idx)


class _Op:
    __slots__ = ("eng", "emit", "deps", "sig", "sidx", "chan", "waits")


class Prog:
    def __init__(self, nc):
        self.nc = nc
        self.ops = []
        self.lastw = {}
        self.readers = {}
        self.frozen = set()

    def freeze(self, *keys):
        for k in keys:
            self.frozen.add(k)

    def add(self, eng, emit, reads=(), writes=(), chan=None):
        op = _Op()
        op.eng = eng
        op.emit = emit
        op.chan = chan
        op.sig = chan is not None
        op.sidx = 0
        op.waits = None
        idx = len(self.ops)
        stream = chan if chan is not None else eng
        deps = set()
        for r in reads:
            w = self.lastw.get(r)
            if w is not None:
                deps.add(w)
        for k in writes:
            w = self.lastw.get(k)
            if w is not None:
                deps.add(w)
            rd = self.readers.get(k)
            if rd:
                deps.update(rd.values())
        for r in reads:
            if r in self.frozen:
                continue
            self.readers.setdefault(r, {})[stream] = idx
        for k in writes:
            self.lastw[k] = idx
            self.readers[k] = {}
        deps.discard(idx)
        op.deps = deps
        self.ops.append(op)
        return idx

    def finalize(self):
        ops = self.ops
        for op in ops:
            for d in op.deps:
                dop = ops[d]
                if dop.chan is None and dop.eng == "pe" and op.eng == "pe" and op.chan is None:
                    continue
                dop.sig = True
        cnt = {}
        for op in ops:
            if op.sig:
                s = op.chan if op.chan is not None else op.eng
                cnt[s] = cnt.get(s, 0) + 1
                op.sidx = cnt[s]
        waited = {}
        for op in ops:
            wd = waited.setdefault(op.eng, {})
            need = {}
            for d in op.deps:
                dop = ops[d]
                if not dop.sig:
                    continue
                if dop.chan is None and dop.eng == "pe" and op.eng == "pe" and op.chan is None:
                    continue
                if dop.chan is not None:
                    s, v = dop.chan, 16 * dop.sidx
                else:
                    s, v = dop.eng, dop.sidx
                if wd.get(s, 0) >= v:
                    continue
                if need.get(s, 0) < v:
                    need[s] = v
            for s, v in need.items():
                wd[s] = v
            op.waits = list(need.items())
        self.streams = sorted(cnt.keys())
        return cnt

    def emit_all(self, block, sems, final_waits):
        nc = self.nc
        ops = self.ops
        engs = {"pe": block.tensor, "act": block.scalar, "dve": block.vector, "pool": block.gpsimd,
                "sp": block.sync}
        for ename, deco in engs.items():
            mine = [op for op in ops if op.eng == ename]

            def body(e, mine=mine, ename=ename):
                for op in mine:
                    for s, v in op.waits:
                        e.wait_ge(sems[s], v)
                    ins = op.emit(e)
                    if op.sig:
                        if op.chan is not None:
                            ins.then_inc(sems[op.chan], 16)
                        else:
                            ins.then_inc(sems[ename], 1)
                if ename == "sp":
                    for s, v in final_waits:
                        e.wait_ge(sems[s], v)

            deco(body)


def build_program(T, L, last_to_out=True):
    NT = T // TT
    NKT = T // 128
    nc = bass.Bass("TRN2", target_bir_lowering=False)
    dt = nc.dram_tensor

    x_in = dt("x", [T, D], F32, kind="ExternalInput").ap()
    cT_in = dt("cT", [128, 16], F32, kind="ExternalInput").ap()
    w_ada_in = dt("w_ada", [L, D, 3 * D], F32, kind="ExternalInput").ap()
    b_ada_in = dt("b_ada_t", [128, L * 48], F32, kind="ExternalInput").ap()
    g_pre_in = dt("g_pre_t", [128, L * 16], F32, kind="ExternalInput").ap()
    g_post_in = dt("g_post_t", [128, L * 16], F32, kind="ExternalInput").ap()
    w_in_in = dt("w_in", [L, D, INW], F32, kind="ExternalInput").ap()
    conv_w_in = dt("conv_w_t", [128, L * 8 * KCONV], F32, kind="ExternalInput").ap()
    conv_b_in = dt("conv_b_t", [128, L * 8], F32, kind="ExternalInput").ap()
    cn_g_in = dt("cn_g_t", [128, L * 8], F32, kind="ExternalInput").ap()
    cn_b_in = dt("cn_b_t", [128, L * 8], F32, kind="ExternalInput").ap()
    w_co_in = dt("w_conv_out", [L, CW, D], F32, kind="ExternalInput").ap()
    lam_in = dt("lam_t", [128, L * 4 * HD], F32, kind="ExternalInput").ap()
    subln_in = dt("subln_t", [128, L], F32, kind="ExternalInput").ap()
    w_ao_in = dt("w_attn_out", [L, CW, D], F32, kind="ExternalInput").ap()
    w_o_in = dt("w_o", [L, D, D], F32, kind="ExternalInput").ap()
    ident_in = dt("ident", [128, 128], F32, kind="ExternalInput").ap()
    trimask_in = dt("trimask", [128, 128], F32, kind="ExternalInput").ap()
    kaug_in = dt("kaug", [128, 2048], F32, kind="ExternalInput").ap()
    qaug_in = dt("qaug", [128, NH * 512], F32, kind="ExternalInput").ap()
    btab_in = dt("btab", [128, NH * NREL], F32, kind="ExternalInput").ap()

    out_d = dt("out", [T, D], F32, kind="ExternalOutput").ap()

    xT_d = dt("xT_s", [D, T], F32, kind="Internal").ap()
    wi_d = dt("wi_s", [L * NBLK_IN * 128, 16 * 512], BF16, kind="Internal").ap()
    wco_d = dt("wco_s", [L * 4 * 128, 8 * 512], BF16, kind="Internal").ap()
    wao_d = dt("wao_s", [L * 4 * 128, 8 * 512], BF16, kind="Internal").ap()
    wo_d = dt("wo_s", [L * 4 * 128, 16 * 512], BF16, kind="Internal").ap()
    KT_d = dt("KT_s", [NH * 128, T], BF16, kind="Internal").ap()
    Vh_d = dt("Vh_s", [NH * 128, NKT * 128], BF16, kind="Internal").ap()

    P = Prog(nc)
    sb = nc.alloc_sbuf_tensor
    A32 = sb("A32", [128, 16 * 512], F32).ap()
    U32 = sb("U32", [128, 8 * 544], F32).ap()
    XIN = sb("XIN", [128, 4 * 512], F32).ap()
    HT = sb("HT", [128, 16 * 512], BF16).ap()
    WB = sb("WB", [128, 2 * 8192], BF16).ap()
    SCZ = sb("SCZ", [128, 8 * 512], BF16).ap()
    QT = sb("QT", [128, 16 * 512], BF16).ap()
    AZ = sb("AZ", [128, 8 * 512], BF16).ap()
    MK = sb("MK", [128, 16 * 512], BF16).ap()
    VB = sb("VB", [128, 4 * 1024], BF16).ap()
    EB = sb("EB", [128, 3 * 1024], BF16).ap()
    TF = sb("TF", [128, 6 * 512], F32).ap()
    TB = sb("TB", [128, 4 * 512], BF16).ap()
    STG = sb("STG", [128, 6 * 512], BF16).ap()
    HALO = sb("HALO", [128, 8 * 32], F32).ap()
    ONES = sb("ONES", [128, 128], BF16).ap()
    IDENT = sb("IDENT", [128, 128], F32).ap()
    TRI = sb("TRI", [128, 128], BF16).ap()
    KAUG = sb("KAUGC", [128, 2048], BF16).ap()
    BTAB = sb("BTAB", [128, NH * NREL], F32).ap()
    CACT = sb("CACT", [128, 16], F32).ap()
    MOD = sb("MOD", [128, L * 48], F32).ap()
    GM = sb("GM", [128, L * 16], F32).ap()
    GG = sb("GG", [128, L * 16], F32).ap()
    GPRE = sb("GPRE", [128, L * 16], F32).ap()
    GPOST = sb("GPOST", [128, L * 16], F32).ap()
    BADA = sb("BADA", [128, L * 48], F32).ap()
    CONVW = sb("CONVW", [128, L * 8 * KCONV], F32).ap()
    CONVB = sb("CONVB", [128, L * 8], F32).ap()
    CNG = sb("CNG", [128, L * 8], F32).ap()
    CNB = sb("CNB", [128, L * 8], F32).ap()
    LAMS = sb("LAMS", [128, 8], F32).ap()
    NLAM = sb("NLAM", [128, L], F32).ap()
    GSUB = sb("GSUB", [128, L], F32).ap()
    EPSC_T = sb("EPSC", [128, 1], F32).ap()
    EPSC = EPSC_T[:, 0:1]
    LAMV = TF[:, 1024:1024 + L * 4 * HD]
    LAMT = TF[:, 2048:2048 + 2 * HD]
    PS = nc.alloc_psum_tensor("PS", [128, 8 * 512], F32).ap()

    def ps(bank, n=512, off=0):
        return PS[:, bank * 512 + off: bank * 512 + off + n]

    def tf(i):
        return TF[:, i * 512:(i + 1) * 512]

    def tb(i):
        return TB[:, i * 512:(i + 1) * 512]

    ctr = {}

    def rr(name, n):
        v = ctr.get(name, 0)
        ctr[name] = v + 1
        return v % n

    def dma(q, out, in_, reads, writes, chan):
        P.add(q, lambda e, out=out, in_=in_: e.dma_start(out=out, in_=in_), reads, writes, chan=chan)

    def act(out, in_, func, reads, writes, bias=None, scale=None):
        kw = {}
        if bias is not None:
            kw["bias"] = bias
        if scale is not None:
            kw["scale"] = scale
        P.add("act", lambda e, out=out, in_=in_, func=func, kw=kw: e.activation(out, in_, func, **kw),
              reads, writes)

    def tt(out, in0, in1, op, reads, writes, eng="dve"):
        P.add(eng, lambda e, out=out, in0=in0, in1=in1, op=op: e.tensor_tensor(out, in0, in1, op),
              reads, writes)

    def ts(out, in0, s1, s2, op0, op1, reads, writes, eng="dve"):
        if op1 is None:
            P.add(eng, lambda e, out=out, in0=in0, s1=s1, op0=op0: e.tensor_scalar(out, in0, s1, None, op0),
                  reads, writes)
        else:
            P.add(eng, lambda e, out=out, in0=in0, s1=s1, s2=s2, op0=op0, op1=op1:
                  e.tensor_scalar(out, in0, s1, s2, op0, op1), reads, writes)

    def recip(out, in_, reads, writes):
        P.add("dve", lambda e, out=out, in_=in_: e.reciprocal(out, in_), reads, writes)

    def stt(out, in0, sc, in1, op0, op1, reads, writes, eng="dve"):
        P.add(eng, lambda e, out=out, in0=in0, sc=sc, in1=in1, op0=op0, op1=op1:
              e.scalar_tensor_tensor(out, in0, sc, in1, op0, op1), reads, writes)

    def mm(out, lhsT, rhs, start, stop, reads, writes):
        P.add("pe", lambda e, out=out, lhsT=lhsT, rhs=rhs, start=start, stop=stop:
              e.matmul(out, lhsT, rhs, start=start, stop=stop), reads, writes)

    def tr(out, in_, reads, writes):
        P.add("pe", lambda e, out=out, in_=in_: e.transpose(out, in_, IDENT), reads, writes)

    dma("sp", IDENT, ident_in, (), (("IDENT",),), "c0")
    dma("sp", BTAB, btab_in, (), (("BTAB",),), "c1")
    dma("sp", tf(0)[:, 0:128], trimask_in, (), (("TF", 0),), "c2")
    act(TRI, tf(0)[:, 0:128], AF.Copy, (("TF", 0),), (("TRI",),))
    P.add("dve", lambda e: e.memset(ONES, 1.0), (), (("ONES",),))
    P.add("dve", lambda e: e.memset(EPSC_T, EPS), (), (("EPSC",),))
    P.add("dve", lambda e: e.memset(HALO, 0.0), (), [("HALO", c) for c in range(8)])
    dma("sp", A32[:, 0:2048], kaug_in, (), [("A32", k) for k in range(4)], "A32")
    act(KAUG[64:67, :], A32[64:67, 0:2048], AF.Copy, [("A32", k) for k in range(4)], (("KAUG",),))
    dma("sp", A32[:, 4096:8192], qaug_in, (), [("A32", k) for k in range(8, 16)], "c13")
    for h in range(NH):
        for m in range(2):
            act(QT[64:67, (2 * h + m) * 512:(2 * h + m + 1) * 512], A32[64:67, 4096 + h * 512:4096 + (h + 1) * 512],
                AF.Copy, [("A32", k) for k in range(8, 16)], (("QTaug", h, m),))
    for (dst, src, key, ch) in ((GPRE, g_pre_in, "GPRE", "c3"), (GPOST, g_post_in, "GPOST", "c4"),
                                (BADA, b_ada_in, "BADA", "c5"), (CONVW, conv_w_in, "CONVW", "c6"),
                                (CONVB, conv_b_in, "CONVB", "c7"), (CNG, cn_g_in, "CNG", "c8"),
                                (CNB, cn_b_in, "CNB", "c9"), (LAMV, lam_in, "LAMVX", "c10"),
                                (GSUB, subln_in, "GSUBraw", "c11"), (CACT, cT_in, "CACTraw", "c12")):
        dma("sp", dst, src, (), ((key,),) if key != "LAMVX" else (("TF", 2), ("TF", 3)), ch)
    act(CACT, CACT, AF.Silu, (("CACTraw",),), (("CACT",),))
    for l in range(L):
        lam0 = lambda_init(l)
        b0 = l * 4 * HD
        tt(LAMT[:, 0:HD], LAMV[:, b0:b0 + HD], LAMV[:, b0 + HD:b0 + 2 * HD], ALU.mult, (("TF", 2), ("TF", 3)), (("TF", 4),))
        tt(LAMT[:, HD:2 * HD], LAMV[:, b0 + 2 * HD:b0 + 3 * HD], LAMV[:, b0 + 3 * HD:b0 + 4 * HD], ALU.mult,
           (("TF", 2), ("TF", 3), ("TF", 4)), (("TF", 4),))
        P.add("dve", lambda e: e.reduce_sum(LAMS[:, 0:1], LAMT[:, 0:HD], mybir.AxisListType.X),
              (("TF", 4),), (("LAMS",),))
        P.add("dve", lambda e: e.reduce_sum(LAMS[:, 1:2], LAMT[:, HD:2 * HD], mybir.AxisListType.X),
              (("TF", 4), ("LAMS",)), (("LAMS",),))
        act(LAMS[:, 2:4], LAMS[:, 0:2], AF.Exp, (("LAMS",),), (("LAMS",),))
        stt(NLAM[:, l:l + 1], LAMS[:, 3:4], -lam0, LAMS[:, 2:3], ALU.add, ALU.subtract,
            (("LAMS",),), (("NLAM", l),))
        ts(GSUB[:, l:l + 1], GSUB[:, l:l + 1], 1.0 - lam0, None, ALU.mult, None,
           (("GSUBraw",), ("GSUB", l - 1)), (("GSUB", l),))
    P.freeze(("EPSC",), ("IDENT",), ("BTAB",), ("TRI",), ("ONES",), ("KAUG",), ("CACT",), ("GPRE",), ("GPOST",), ("BADA",),
             ("CONVW",), ("CONVB",), ("CNG",), ("CNB",))

    for l in range(L):
        for grp in range(12):
            src = w_ada_in[l].rearrange("(kc p) n -> p kc n", p=128)[:, :, grp * 512:(grp + 1) * 512]
            dma("sp", A32.rearrange("p (kc n) -> p kc n", kc=16), src, (), [("A32", k) for k in range(16)], "A32")
            for s in range(4):
                oc = grp * 4 + s
                bank = rr("ps", 8)
                for kc in range(16):
                    mm(ps(bank, 1), A32[:, kc * 512 + s * 128: kc * 512 + (s + 1) * 128], CACT[:, kc:kc + 1],
                       kc == 0, kc == 15, [("A32", k) for k in range(16)] + [("CACT",)], (("ps", bank),))
                tt(MOD[:, l * 48 + oc: l * 48 + oc + 1], ps(bank, 1), BADA[:, l * 48 + oc: l * 48 + oc + 1], ALU.add,
                   (("ps", bank), ("BADA",)), (("MOD", l),))
        stt(GM[:, l * 16:(l + 1) * 16], MOD[:, l * 48 + 16:l * 48 + 32], 1.0, GPRE[:, l * 16:(l + 1) * 16],
            ALU.add, ALU.mult, (("MOD", l), ("GPRE",)), (("GM", l),))
        tt(GG[:, l * 16:(l + 1) * 16], MOD[:, l * 48 + 32:l * 48 + 48], GPOST[:, l * 16:(l + 1) * 16], ALU.mult,
           (("MOD", l), ("GPOST",)), (("GG", l),))

    cvn = [0]

    def convert(src_rows, dst_view, ncols):
        i = cvn[0]
        cvn[0] += 1
        s32 = i % 2
        sbf = i % 2
        stage = A32[:, s32 * 2048: s32 * 2048 + ncols]
        stb = HT[:, sbf * 2048: sbf * 2048 + ncols]
        dma("sp", stage, src_rows, (), (("A32", "cv", s32),), "cvl%d" % s32)
        if i % 2 == 0:
            act(stb, stage, AF.Copy, (("A32", "cv", s32),), (("HT", "cv", sbf),))
        else:
            P.add("dve", lambda e, stb=stb, stage=stage: e.tensor_copy(stb, stage),
                  (("A32", "cv", s32),), (("HT", "cv", sbf),))
        dma("pool", dst_view, stb.rearrange("p (b c) -> p b c", c=512), (("HT", "cv", sbf),),
            (("WSCRP", sbf),), "cvs%d" % sbf)

    P.add("dve", lambda e: e.memset(TF[:, 0:1], 0.0), (),
          [("A32", k) for k in range(16)] + [("A32", "cv", 0), ("A32", "cv", 1), ("TF", 0)])
    for l in range(L):
        wv = wi_d[l * NBLK_IN * 128:(l + 1) * NBLK_IN * 128, :].rearrange("(b p) c -> p b c", p=128)
        for kc in range(16):
            for pc in range(NBLK_IN // 2):
                convert(w_in_in[l, kc * 128:(kc + 1) * 128, pc * 1024:(pc + 1) * 1024],
                        wv[:, 2 * pc:2 * pc + 2, kc * 512:(kc + 1) * 512], 1024)
        for (src, dst, nkc) in ((w_co_in, wco_d, 8), (w_ao_in, wao_d, 8), (w_o_in, wo_d, 16)):
            wv2 = dst[l * 4 * 128:(l + 1) * 4 * 128, :].rearrange("(b p) c -> p b c", p=128)
            for kc in range(nkc):
                for pc in range(2):
                    convert(src[l, kc * 128:(kc + 1) * 128, pc * 1024:(pc + 1) * 1024],
                            wv2[:, 2 * pc:2 * pc + 2, kc * 512:(kc + 1) * 512], 1024)
    P.add("dve", lambda e: e.memset(TF[:, 0:1], 0.0), (),
          [("A32", "cv", 0), ("A32", "cv", 1), ("HT", "cv", 0), ("HT", "cv", 1), ("TF", 0)]
          + [("A32", k) for k in range(16)] + [("HT", k) for k in range(16)])

    def load_wblock(src_rows, nkc):
        slot = rr("wb", 2)
        view = WB[:, slot * 8192: slot * 8192 + nkc * 512]
        dma("sp", view, src_rows[:, 0:nkc * 512], (("WSCRP", 0), ("WSCRP", 1)), (("WB", slot),), "wb%d" % slot)
        return slot

    def wblk(slot, kc, s):
        return WB[:, slot * 8192 + kc * 512 + s * 128: slot * 8192 + kc * 512 + (s + 1) * 128]

    SPAN = 8
    NSLOT = 4
    HTall = [("HT", k) for k in range(16)]

    def wi_rows(l, blk):
        r0 = (l * NBLK_IN + blk) * 128
        return wi_d[r0:r0 + 128, :]

    def proj_fm(s, slot, bank):
        for kc in range(16):
            mm(ps(bank), wblk(slot, kc, s), HT[:, kc * 512:(kc + 1) * 512], kc == 0, kc == 15,
               (("WB", slot), ("HT", kc)), (("ps", bank),))

    def rstd_from(bank, scale, dst, dstk):
        act(dst, ps(bank), AF.Sqrt, (("ps", bank), ("EPSC",)), (dstk,), bias=EPSC, scale=scale)
        recip(dst, dst, (dstk,), (dstk,))

    def phaseA_first(l, i):
        t0 = i * TT
        for g in range(4):
            xs = rr("xtok", 2)
            xtok = U32[:, xs * 2176: xs * 2176 + 2048]
            xk = [("U32", c) for c in range(xs * 4, xs * 4 + 4)]
            dma("sp", xtok, x_in[t0 + g * 128: t0 + (g + 1) * 128, :], (), xk, "xtok%d" % xs)
            for kq in range(4):
                bank = rr("ps", 8)
                for j in range(4):
                    kc = kq * 4 + j
                    tr(ps(bank, 128, j * 128), xtok[:, kc * 128:(kc + 1) * 128], xk + [("IDENT",)], (("ps", bank),))
                dst = A32.rearrange("p (kc t) -> p kc t", kc=16)[:, kq * 4:kq * 4 + 4, g * 128:(g + 1) * 128]
                srcp = ps(bank).rearrange("p (j t) -> p j t", j=4)
                if (g + kq) % 2 == 0:
                    act(dst, srcp, AF.Copy, (("ps", bank),), [("A32", kq * 4 + j) for j in range(4)])
                else:
                    P.add("dve", lambda e, dst=dst, srcp=srcp: e.tensor_copy(dst, srcp), (("ps", bank),),
                          [("A32", kq * 4 + j) for j in range(4)])
        dma("pool", xT_d.rearrange("(kc p) t -> p kc t", p=128)[:, :, t0:t0 + TT],
            A32.rearrange("p (kc t) -> p kc t", kc=16), [("A32", k) for k in range(16)], (("xT", i),), "A32st")
        bank = rr("ps", 8)
        for kc in range(16):
            tbi = rr("tb", 4)
            act(tb(tbi), A32[:, kc * 512:(kc + 1) * 512], AF.Square, (("A32", kc),), (("TB", tbi),))
            mm(ps(bank), ONES, tb(tbi), kc == 0, kc == 15, (("TB", tbi), ("ONES",)), (("ps", bank),))
        rstd_from(bank, 1.0 / D, tf(0), ("TF", 0))
        for kc in range(16):
            tfi = 1 + rr("tfA", 2)
            tt(tf(tfi), A32[:, kc * 512:(kc + 1) * 512], tf(0), ALU.mult, (("A32", kc), ("TF", 0)), (("TF", tfi),))
            act(HT[:, kc * 512:(kc + 1) * 512], tf(tfi), AF.Identity, (("TF", tfi), ("GM", l), ("MOD", l)),
                (("HT", kc),), bias=MOD[:, l * 48 + kc: l * 48 + kc + 1], scale=GM[:, l * 16 + kc: l * 16 + kc + 1])

    def phaseA_stream(l, i):
        t0 = i * TT
        bank = rr("ps", 8)
        for kc in range(16):
            xs = rr("xin", 4)
            xin = XIN[:, xs * 512:(xs + 1) * 512]
            dma("sp", xin, xT_d[kc * 128:(kc + 1) * 128, t0:t0 + TT], (("xT", i),), (("XIN", xs),), "xin%d" % xs)
            tbi = rr("tb", 4)
            act(tb(tbi), xin, AF.Square, (("XIN", xs),), (("TB", tbi),))
            mm(ps(bank), ONES, tb(tbi), kc == 0, kc == 15, (("TB", tbi), ("ONES",)), (("ps", bank),))
        rstd_from(bank, 1.0 / D, tf(5), ("TF", 5))
        for kc in range(16):
            xs = rr("xin", 4)
            xin = XIN[:, xs * 512:(xs + 1) * 512]
            dma("sp", xin, xT_d[kc * 128:(kc + 1) * 128, t0:t0 + TT], (("xT", i),), (("XIN", xs),), "xin%d" % xs)
            tt(xin, xin, tf(5), ALU.mult, (("XIN", xs), ("TF", 5)), (("XIN", xs),))
            act(HT[:, kc * 512:(kc + 1) * 512], xin, AF.Identity, (("XIN", xs), ("GM", l), ("MOD", l)),
                (("HT", kc),), bias=MOD[:, l * 48 + kc: l * 48 + kc + 1], scale=GM[:, l * 16 + kc: l * 16 + kc + 1])

    def conv_chunk(l, i, c):
        ub = c * 544
        if i == 0:
            P.add("dve", lambda e, ub=ub: e.memset(U32[:, ub:ub + 30], 0.0), (("U32", c),), (("U32", c),))
        else:
            P.add("dve", lambda e, ub=ub, c=c: e.tensor_copy(U32[:, ub:ub + 30], HALO[:, c * 32:c * 32 + 30]),
                  (("HALO", c), ("U32", c)), (("U32", c),))
        P.add("dve", lambda e, ub=ub, c=c: e.tensor_copy(HALO[:, c * 32:c * 32 + 30], U32[:, ub + 512:ub + 542]),
              (("U32", c),), (("HALO", c),))
        wb0 = (l * 8 + c) * KCONV
        accs = (tf(1), tf(2))
        acck = (("TF", 1), ("TF", 2))
        ts(accs[0], U32[:, ub:ub + 512], CONVW[:, wb0:wb0 + 1], CONVB[:, l * 8 + c:l * 8 + c + 1], ALU.mult, ALU.add,
           (("U32", c), ("CONVW",), ("CONVB",)), (acck[0],))
        ts(accs[1], U32[:, ub + 1:ub + 513], CONVW[:, wb0 + 1:wb0 + 2], None, ALU.mult, None,
           (("U32", c), ("CONVW",)), (acck[1],))
        for j in range(2, KCONV):
            a = j % 2
            stt(accs[a], U32[:, ub + j:ub + j + 512], CONVW[:, wb0 + j:wb0 + j + 1], accs[a], ALU.mult, ALU.add,
                (("U32", c), ("CONVW",), acck[a]), (acck[a],))
        tt(U32[:, ub + 30:ub + 542], accs[0], accs[1], ALU.add, (acck[0], acck[1]), (("U32", c),))

    def phaseBE(l, i):
        t0 = i * TT
        for blk in range(14):
            slot = load_wblock(wi_rows(l, blk), 16)
            if blk in (10, 11):
                for g in range(4):
                    bank = rr("ps", 8)
                    for kc in range(16):
                        mm(ps(bank), HT[:, kc * 512 + g * 128: kc * 512 + (g + 1) * 128],
                           WB[:, slot * 8192 + kc * 512: slot * 8192 + (kc + 1) * 512], kc == 0, kc == 15,
                           (("WB", slot), ("HT", kc)), (("ps", bank),))
                    st = rr("stg", 6)
                    stg = STG[:, st * 512:(st + 1) * 512]
                    act(stg, ps(bank), AF.Copy, (("ps", bank),), (("STG", st),))
                    kt = i * 4 + g
                    h0 = (blk - 10) * 4
                    dstv = Vh_d.rearrange("(h p) (n d) -> p h n d", p=128, d=128)[:, h0:h0 + 4, kt, :]
                    dma("pool", dstv, stg.rearrange("p (h d) -> p h d", h=4), (("STG", st),),
                        [("Vh", h0 + hh, i, g) for hh in range(4)], "stg%d" % st)
                continue
            for s in range(4):
                bank = rr("ps", 8)
                proj_fm(s, slot, bank)
                c = (blk % 2) * 4 + s
                if blk in (0, 1):
                    act(U32[:, c * 544 + 30: c * 544 + 542], ps(bank), AF.Copy, (("ps", bank),), (("U32", c),))
                elif blk in (2, 3):
                    tfi = 3 + rr("tfB", 2)
                    act(tf(tfi), ps(bank), AF.Sigmoid, (("ps", bank),), (("TF", tfi),))
                    tt(U32[:, c * 544 + 30: c * 544 + 542], U32[:, c * 544 + 30: c * 544 + 542], tf(tfi), ALU.mult,
                       (("U32", c), ("TF", tfi)), (("U32", c),))
                elif blk in (4, 5):
                    act(SCZ[:, c * 512:(c + 1) * 512], ps(bank), AF.Silu, (("ps", bank),), (("SCZ", c),))
                elif blk in (6, 7):
                    act(QT[0:64, (2 * c) * 512:(2 * c + 1) * 512], ps(bank)[0:64, :], AF.Identity, (("ps", bank),),
                        (("QT", c, 0),), scale=HD ** -0.5)
                    st = rr("stg", 6)
                    stg = STG[:, st * 512:(st + 1) * 512]
                    act(stg[64:128, :], ps(bank)[64:128, :], AF.Identity, (("ps", bank),), (("STG", st),),
                        scale=HD ** -0.5)
                    dma("pool", QT[0:64, (2 * c + 1) * 512:(2 * c + 2) * 512], stg[64:128, :], (("STG", st),),
                        (("QT", c, 1),), "stg%d" % st)
                elif blk in (8, 9):
                    st = rr("stg", 6)
                    stg = STG[:, st * 512:(st + 1) * 512]
                    act(stg, ps(bank), AF.Copy, (("ps", bank),), (("STG", st),))
                    dma("pool", KT_d[c * 128:(c + 1) * 128, t0:t0 + TT], stg, (("STG", st),), (("KT", c, i),),
                        "stg%d" % st)
                elif blk in (12, 13):
                    act(AZ[:, c * 512:(c + 1) * 512], ps(bank), AF.Silu, (("ps", bank),), (("AZ", c),))
            if blk == 3:
                for c in range(8):
                    conv_chunk(l, i, c)

        bank_m = rr("ps", 8)
        bank_q = rr("ps", 8)
        for c in range(8):
            ub = c * 544
            t1 = rr("tb", 4)
            act(tb(t1), U32[:, ub + 30:ub + 542], AF.Copy, (("U32", c),), (("TB", t1),))
            mm(ps(bank_m), ONES, tb(t1), c == 0, c == 7, (("TB", t1), ("ONES",)), (("ps", bank_m),))
            t2 = rr("tb", 4)
            act(tb(t2), U32[:, ub + 30:ub + 542], AF.Square, (("U32", c),), (("TB", t2),))
            mm(ps(bank_q), ONES, tb(t2), c == 0, c == 7, (("TB", t2), ("ONES",)), (("ps", bank_q),))
        ts(tf(3), ps(bank_m), 1.0 / CW, None, ALU.mult, None, (("ps", bank_m),), (("TF", 3),))
        tt(tf(0), tf(3), tf(3), ALU.mult, (("TF", 3),), (("TF", 0),))
        stt(tf(4), ps(bank_q), 1.0 / CW, tf(0), ALU.mult, ALU.subtract, (("ps", bank_q), ("TF", 0)), (("TF", 4),))
        act(tf(4), tf(4), AF.Sqrt, (("TF", 4), ("EPSC",)), (("TF", 4),), bias=EPSC, scale=1.0)
        recip(tf(4), tf(4), (("TF", 4),), (("TF", 4),))
        for c in range(8):
            ub = c * 544
            tfi = 1 + rr("tfA", 2)
            tt(tf(tfi), U32[:, ub + 30:ub + 542], tf(3), ALU.subtract, (("U32", c), ("TF", 3)), (("TF", tfi),))
            tt(tf(tfi), tf(tfi), tf(4), ALU.mult, (("TF", tfi), ("TF", 4)), (("TF", tfi),))
            act(tf(tfi), tf(tfi), AF.Silu, (("TF", tfi), ("CNG",), ("CNB",)), (("TF", tfi),),
                bias=CNB[:, l * 8 + c:l * 8 + c + 1], scale=CNG[:, l * 8 + c:l * 8 + c + 1])
            tt(SCZ[:, c * 512:(c + 1) * 512], tf(tfi), SCZ[:, c * 512:(c + 1) * 512], ALU.mult,
               (("TF", tfi), ("SCZ", c)), (("SCZ", c),))

        for q4 in range(4):
            act(MK[64:67, q4 * 2048:(q4 + 1) * 2048], KAUG[64:67, :], AF.Copy,
                [("MK", q4 * 4 + j) for j in range(4)] + [("KAUG",)], [("MK", q4 * 4 + j) for j in range(4)])
        nkt = 4 * (i + 1)
        nspan = (nkt + SPAN - 1) // SPAN
        for h in range(NH):
            blocks = []
            for sp in range(nspan):
                ntl = min(SPAN, nkt - sp * SPAN)
                for ktl in range(ntl):
                    blocks.append((sp, ktl, ntl))
            cur = {}

            def issue_loads(sp, ntl):
                sl = rr("kb", NSLOT)
                cur[sp] = sl
                tl0 = (sp * SPAN) // 4
                tiles = range(tl0, tl0 + (ntl + 3) // 4)
                kkeys = [("MK", sl * 4 + j) for j in range(4)]
                for m in range(2):
                    base = sl * 2048 + m * 1024
                    dma("sp", MK[0:64, base: base + ntl * 128],
                        KT_d[h * 128 + m * 64: h * 128 + (m + 1) * 64, sp * SPAN * 128: sp * SPAN * 128 + ntl * 128],
                        [("KT", h, tq) for tq in tiles], [("MK", sl * 4 + 2 * m), ("MK", sl * 4 + 2 * m + 1)],
                        "kb%d%d" % (sl, m))
                dma("sp", VB[:, sl * 1024: sl * 1024 + ntl * 128],
                    Vh_d[h * 128:(h + 1) * 128, sp * SPAN * 128: sp * SPAN * 128 + ntl * 128],
                    [("Vh", h, tq, g) for tq in tiles for g in range(4)], (("VB", sl),), "vb%d" % sl)

            def qk_exp(bi):
                sp, ktl, ntl = blocks[bi]
                if ktl == 0:
                    issue_loads(sp, ntl)
                sl = cur[sp]
                kt = sp * SPAN + ktl
                mdiag = kt - 4 * i
                c0 = 128 * max(mdiag, 0)
                n = 512 - c0
                rel = 4 * i - kt
                sb2 = bi % 2
                eb = bi % 3
                for m in range(2):
                    base = sl * 2048 + m * 1024
                    mm(ps(sb2 * 2 + m, n, c0), MK[0:67, base + ktl * 128: base + (ktl + 1) * 128],
                       QT[0:67, (2 * h + m) * 512 + c0:(2 * h + m + 1) * 512], True, True,
                       [("MK", sl * 4 + 2 * m), ("MK", sl * 4 + 2 * m + 1), ("QT", h, m), ("QTaug", h, m)],
                       (("ps", sb2 * 2 + m),))
                ein = PS[:, sb2 * 1024:(sb2 + 1) * 1024].rearrange("p (m q) -> p m q", m=2)[:, :, c0:512]
                eout = EB[:, eb * 1024:(eb + 1) * 1024].rearrange("p (m q) -> p m q", m=2)[:, :, c0:512]
                act(eout, ein, AF.Exp, (("ps", sb2 * 2), ("ps", sb2 * 2 + 1), ("BTAB",)), (("EB", eb),),
                    bias=BTAB[:, h * NREL + rel + 8: h * NREL + rel + 9], scale=1.0)
                if mdiag >= 0:
                    for m in range(2):
                        blkv = EB[:, eb * 1024 + m * 512 + c0: eb * 1024 + m * 512 + c0 + 128]
                        tt(blkv, blkv, TRI, ALU.mult, (("EB", eb), ("TRI",)), (("EB", eb),))

            def pv(bi):
                sp, ktl, ntl = blocks[bi]
                sl = cur[sp]
                kt = sp * SPAN + ktl
                c0 = 128 * max(kt - 4 * i, 0)
                n = 512 - c0
                eb = bi % 3
                first = (kt == 0)
                last = (kt == nkt - 1)
                for m in range(2):
                    erhs = EB[:, eb * 1024 + m * 512 + c0: eb * 1024 + (m + 1) * 512]
                    mm(ps(4 + m, n, c0), VB[:, sl * 1024 + ktl * 128: sl * 1024 + (ktl + 1) * 128], erhs,
                       first, last, (("VB", sl), ("EB", eb)), (("ps", 4 + m),))
                    mm(ps(6 + m, n, c0), ONES, erhs, first, last, (("EB", eb), ("ONES",)), (("ps", 6 + m),))

            nb = len(blocks)
            for bi in range(nb + 1):
                if bi < nb:
                    qk_exp(bi)
                if bi >= 1:
                    pv(bi - 1)
            recip(tf(1), ps(6), (("ps", 6),), (("TF", 1),))
            recip(tf(2), ps(7), (("ps", 7),), (("TF", 2),))
            tt(tf(1), ps(4), tf(1), ALU.mult, (("ps", 4), ("TF", 1)), (("TF", 1),))
            tt(tf(2), ps(5), tf(2), ALU.mult, (("ps", 5), ("TF", 2)), (("TF", 2),))
            stt(tf(1), tf(2), NLAM[:, l:l + 1], tf(1), ALU.mult, ALU.add, (("TF", 1), ("TF", 2), ("NLAM", l)),
                (("TF", 1),))
            t1 = rr("tb", 4)
            act(tb(t1), tf(1), AF.Square, (("TF", 1),), (("TB", t1),))
            mm(ps(6), ONES, tb(t1), True, True, (("TB", t1), ("ONES",)), (("ps", 6),))
            rstd_from(6, 1.0 / VD, tf(2), ("TF", 2))
            tt(tf(1), tf(1), tf(2), ALU.mult, (("TF", 1), ("TF", 2)), (("TF", 1),))
            stt(AZ[:, h * 512:(h + 1) * 512], tf(1), GSUB[:, l:l + 1], AZ[:, h * 512:(h + 1) * 512], ALU.mult,
                ALU.mult, (("TF", 1), ("GSUB", l), ("AZ", h)), (("AZ", h),))

        GC = U32[:, 0:2048]
        GA = U32[:, 2176:2176 + 2048]
        gck = [("U32", c) for c in range(4)]
        gak = [("U32", c) for c in range(4, 8)]
        for ob in range(4):
            slot = load_wblock(wi_rows(l, 14 + ob), 16)
            for s in range(4):
                bank = rr("ps", 8)
                proj_fm(s, slot, bank)
                act(GC[:, s * 512:(s + 1) * 512], ps(bank), AF.Sigmoid, (("ps", bank),), gck)
            slot = load_wblock(wi_rows(l, 18 + ob), 16)
            for s in range(4):
                bank = rr("ps", 8)
                proj_fm(s, slot, bank)
                act(GA[:, s * 512:(s + 1) * 512], ps(bank), AF.Sigmoid, (("ps", bank),), gak)
            r0 = (l * 4 + ob) * 128
            slot = load_wblock(wco_d[r0:r0 + 128, :], 8)
            for s in range(4):
                bank = rr("ps", 8)
                for kc in range(8):
                    mm(ps(bank), wblk(slot, kc, s), SCZ[:, kc * 512:(kc + 1) * 512], kc == 0, kc == 7,
                       (("WB", slot), ("SCZ", kc)), (("ps", bank),))
                tt(GC[:, s * 512:(s + 1) * 512], ps(bank), GC[:, s * 512:(s + 1) * 512], ALU.mult,
                   [("ps", bank)] + gck, gck)
            slot = load_wblock(wao_d[r0:r0 + 128, :], 8)
            for s in range(4):
                bank = rr("ps", 8)
                for kc in range(8):
                    mm(ps(bank), wblk(slot, kc, s), AZ[:, kc * 512:(kc + 1) * 512], kc == 0, kc == 7,
                       (("WB", slot), ("AZ", kc)), (("ps", bank),))
                tt(GA[:, s * 512:(s + 1) * 512], ps(bank), GA[:, s * 512:(s + 1) * 512], ALU.mult,
                   [("ps", bank)] + gak, gak)
                oc = ob * 4 + s
                tt(MK[:, oc * 512:(oc + 1) * 512], GC[:, s * 512:(s + 1) * 512], GA[:, s * 512:(s + 1) * 512],
                   ALU.add, gck + gak, (("MK", oc),))

    def phaseF(l, i):
        t0 = i * TT
        lastl = (l == L - 1)
        bank_o = rr("ps", 8)
        for ob in range(4):
            r0 = (l * 4 + ob) * 128
            slot = load_wblock(wo_d[r0:r0 + 128, :], 16)
            for s in range(4):
                oc = ob * 4 + s
                bank = rr("ps", 8)
                if bank == bank_o:
                    bank = rr("ps", 8)
                for kc in range(16):
                    mm(ps(bank), wblk(slot, kc, s), MK[:, kc * 512:(kc + 1) * 512], kc == 0, kc == 15,
                       (("WB", slot), ("MK", kc)), (("ps", bank),))
                act(A32[:, oc * 512:(oc + 1) * 512], ps(bank), AF.Copy, (("ps", bank),), (("A32", oc),))
                t1 = rr("tb", 4)
                act(tb(t1), ps(bank), AF.Square, (("ps", bank),), (("TB", t1),))
                mm(ps(bank_o), ONES, tb(t1), oc == 0, oc == 15, (("TB", t1), ("ONES",)), (("ps", bank_o),))
        rstd_from(bank_o, 1.0 / D, tf(0), ("TF", 0))
        for oc in range(16):
            xs = rr("xin", 4)
            xin = XIN[:, xs * 512:(xs + 1) * 512]
            dma("sp", xin, xT_d[oc * 128:(oc + 1) * 128, t0:t0 + TT], (("xT", i),), (("XIN", xs),), "xin%d" % xs)
            tt(A32[:, oc * 512:(oc + 1) * 512], A32[:, oc * 512:(oc + 1) * 512], tf(0), ALU.mult,
               (("A32", oc), ("TF", 0)), (("A32", oc),))
            stt(A32[:, oc * 512:(oc + 1) * 512], A32[:, oc * 512:(oc + 1) * 512], GG[:, l * 16 + oc:l * 16 + oc + 1],
                xin, ALU.mult, ALU.add, (("A32", oc), ("GG", l), ("XIN", xs)), (("A32", oc),))
        if not lastl:
            dma("pool", xT_d.rearrange("(kc p) t -> p kc t", p=128)[:, :, t0:t0 + TT],
                A32.rearrange("p (kc t) -> p kc t", kc=16), [("A32", k) for k in range(16)],
                (("xT", i),), "A32st")
        else:
            for g in range(4):
                osl = rr("ost", 2)
                ost = U32[:, osl * 2176: osl * 2176 + 2048]
                ostk = [("U32", c) for c in range(osl * 4, osl * 4 + 4)]
                for kq in range(4):
                    bank = rr("ps", 8)
                    for j in range(4):
                        kc = kq * 4 + j
                        tr(ps(bank, 128, j * 128), A32[:, kc * 512 + g * 128: kc * 512 + (g + 1) * 128],
                           (("A32", kc), ("IDENT",)), (("ps", bank),))
                    if kq % 2 == 0:
                        act(ost[:, kq * 512:(kq + 1) * 512], ps(bank), AF.Copy, (("ps", bank),), ostk)
                    else:
                        P.add("dve", lambda e, ost=ost, kq=kq, bank=bank:
                              e.tensor_copy(ost[:, kq * 512:(kq + 1) * 512], ps(bank)), (("ps", bank),), ostk)
                dma("pool", out_d[t0 + g * 128: t0 + (g + 1) * 128, :], ost, ostk, (("OUT",),), "ost%d" % osl)

    steps = [(l, i) for l in range(L) for i in range(NT)]
    for sidx, (l, i) in enumerate(steps):
        if l == 0:
            phaseA_first(l, i)
        elif sidx == 0:
            phaseA_stream(l, i)
        phaseBE(l, i)
        if sidx + 1 < len(steps) and steps[sidx + 1][0] >= 1:
            phaseA_stream(*steps[sidx + 1])
        phaseF(l, i)

    cnt = P.finalize()
    sems = {s: nc.alloc_semaphore(name="s_" + s) for s in P.streams}
    final_waits = [(s, 16 * cnt[s]) for s in ("ost0", "ost1") if s in cnt]
    with nc.Block() as block:
        P.emit_all(block, sems, final_waits)
    return nc, len(P.ops)


_ALIBI = [2.0 ** (-8.0 * (h + 1) / NH) for h in range(NH)]


def host_constants():
    ident = np.eye(128, dtype=np.float32)
    kk = np.arange(128)
    trimask = (kk[:, None] <= kk[None, :]).astype(np.float32)
    kaug = np.zeros((128, 2048), np.float32)
    kaug[64, :] = 1.0
    kaug[65, :] = 1.0
    kaug[66, :] = np.arange(2048) % 128
    qaug = np.zeros((128, NH * 512), np.float32)
    ii = np.arange(512)
    for h in range(NH):
        s = _ALIBI[h]
        qaug[64, h * 512:(h + 1) * 512] = -s * (ii % 256)
        qaug[65, h * 512:(h + 1) * 512] = -s * 256.0 * (ii // 256)
        qaug[66, h * 512:(h + 1) * 512] = s
    btab = np.zeros((128, NH * NREL), np.float32)
    for h in range(NH):
        for r in range(NREL):
            btab[:, h * NREL + r] = -_ALIBI[h] * 128.0 * (r - 8)
    return {"ident": ident, "trimask": trimask, "kaug": kaug, "qaug": qaug, "btab": btab}


def host_layout(L, c_b, w_ada, b_ada, g_pre, g_post, w_in, conv_w, conv_b, cn_g, cn_b, w_conv_out,
                lam_q1, lam_k1, lam_q2, lam_k2, subln_g, w_attn_out, w_o):
    f = np.float32
    m = {}
    m["cT"] = np.ascontiguousarray(c_b.reshape(16, 128).T).astype(f)
    m["w_ada"] = np.ascontiguousarray(w_ada[:L])
    m["b_ada_t"] = np.ascontiguousarray(b_ada[:L].reshape(L, 48, 128).transpose(2, 0, 1).reshape(128, L * 48))
    m["g_pre_t"] = np.ascontiguousarray(g_pre[:L].reshape(L, 16, 128).transpose(2, 0, 1).reshape(128, L * 16))
    m["g_post_t"] = np.ascontiguousarray(g_post[:L].reshape(L, 16, 128).transpose(2, 0, 1).reshape(128, L * 16))
    m["w_in"] = np.ascontiguousarray(w_in[:L])
    m["conv_w_t"] = np.ascontiguousarray(
        conv_w[:L].reshape(L, KCONV, 8, 128).transpose(3, 0, 2, 1).reshape(128, L * 8 * KCONV))
    for nm, a in (("conv_b_t", conv_b), ("cn_g_t", cn_g), ("cn_b_t", cn_b)):
        m[nm] = np.ascontiguousarray(a[:L].reshape(L, 8, 128).transpose(2, 0, 1).reshape(128, L * 8))
    m["w_conv_out"] = np.ascontiguousarray(w_conv_out[:L])
    lam = np.stack([lam_q1[:L], lam_k1[:L], lam_q2[:L], lam_k2[:L]], axis=1).reshape(1, L * 4 * HD)
    m["lam_t"] = np.ascontiguousarray(np.broadcast_to(lam, (128, L * 4 * HD)))
    m["subln_t"] = np.ascontiguousarray(subln_g[:L].T)
    m["w_attn_out"] = np.ascontiguousarray(w_attn_out[:L])
    m["w_o"] = np.ascontiguousarray(w_o[:L])
    return {k: np.asarray(v, dtype=f) for k, v in m.items()}


def kernel(x, c, w_ada, b_ada, g_pre, g_post, w_in, conv_w, conv_b, cn_g, cn_b, w_conv_out,
           lam_q1, lam_k1, lam_q2, lam_k2, subln_g, w_attn_out, w_o):
    arrs = [np.asarray(a, dtype=np.float32) for a in
            (x, c, w_ada, b_ada, g_pre, g_post, w_in, conv_w, conv_b, cn_g, cn_b, w_conv_out,
             lam_q1, lam_k1, lam_q2, lam_k2, subln_g, w_attn_out, w_o)]
    x, c = arrs[0], arrs[1]
    B, S, _ = x.shape
    L = arrs[2].shape[0]
    nc, _ = build_program(S, L)
    consts = host_constants()
    in_maps = []
    for core in range(8):
        b = core % B
        mp = host_layout(L, c[b], *arrs[2:])
        mp.update(consts)
        mp["x"] = np.ascontiguousarray(x[b])
        in_maps.append(mp)
    res = run_bass_kernel_spmd(nc, in_maps, core_ids=list(range(8)))
    out = np.stack([np.asarray(res.results[b]["out"], dtype=np.float32) for b in range(B)], axis=0)
    return out
```

```python
import math
import numpy as np
import concourse.bass as bass
import concourse.mybir as mybir
from concourse.bass_utils import run_bass_kernel_spmd

F32 = mybir.dt.float32
BF16 = mybir.dt.bfloat16
ALU = mybir.AluOpType
AF = mybir.ActivationFunctionType

D = 2048
CW = 1024
KCONV = 31
NH = 8
HD = 64
VD = 128
INW = 11264
EPS = 1e-6
TT = 512
NBLK_IN = INW // 512
DEPTH = 4
BATCH = 4
SEQ = 8192
NREL = 80


def lambda_init(layer_idx):
    return 0.8 - 0.6 * math.exp(-0.3 * layer_idx)


class _Op:
    __slots__ = ("eng", "emit", "deps", "sig", "sidx", "chan", "waits")


class Prog:
    def __init__(self, nc):
        self.nc = nc
        self.ops = []
        self.lastw = {}
        self.readers = {}
        self.frozen = set()

    def freeze(self, *keys):
        for k in keys:
            self.frozen.add(k)

    def add(self, eng, emit, reads=(), writes=(), chan=None):
        op = _Op()
        op.eng = eng
        op.emit = emit
        op.chan = chan
        op.sig = chan is not None
        op.sidx = 0
        op.waits = None
        idx = len(self.ops)
        stream = chan if chan is not None else eng
        deps = set()
        for r in reads:
            w = self.lastw.get(r)
            if w is not None:
                deps.add(w)
        for k in writes:
            w = self.lastw.get(k)
            if w is not None:
                deps.add(w)
            rd = self.readers.get(k)
            if rd:
                deps.update(rd.values())
        for r in reads:
            if r in self.frozen:
                continue
            self.readers.setdefault(r, {})[stream] = idx
        for k in writes:
            self.lastw[k] = idx
            self.readers[k] = {}
        deps.discard(idx)
        op.deps = deps
        self.ops.append(op)
        return idx

    def finalize(self):
        ops = self.ops
        for op in ops:
            for d in op.deps:
                dop = ops[d]
                if dop.chan is None and dop.eng == "pe" and op.eng == "pe" and op.chan is None:
                    continue
                dop.sig = True
        cnt = {}
        for op in ops:
            if op.sig:
                s = op.chan if op.chan is not None else op.eng
                cnt[s] = cnt.get(s, 0) + 1
                op.sidx = cnt[s]
        waited = {}
        for op in ops:
            wd = waited.setdefault(op.eng, {})
            need = {}
            for d in op.deps:
                dop = ops[d]
                if not dop.sig:
                    continue
                if dop.chan is None and dop.eng == "pe" and op.eng == "pe" and op.chan is None:
                    continue
                if dop.chan is not None:
                    s, v = dop.chan, 16 * dop.sidx
                else:
                    s, v = dop.eng, dop.sidx
                if wd.get(s, 0) >= v:
                    continue
                if need.get(s, 0) < v:
                    need[s] = v
            for s, v in need.items():
                wd[s] = v
            op.waits = list(need.items())
        self.streams = sorted(cnt.keys())
        return cnt

    def emit_all(self, block, sems, final_waits):
        nc = self.nc
        ops = self.ops
        engs = {"pe": block.tensor, "act": block.scalar, "dve": block.vector, "pool": block.gpsimd,
                "sp": block.sync}
        for ename, deco in engs.items():
            mine = [op for op in ops if op.eng == ename]

            def body(e, mine=mine, ename=ename):
                for op in mine:
                    for s, v in op.waits:
                        e.wait_ge(sems[s], v)
                    ins = op.emit(e)
                    if op.sig:
                        if op.chan is not None:
                            ins.then_inc(sems[op.chan], 16)
                        else:
                            ins.then_inc(sems[ename], 1)
                if ename == "sp":
                    for s, v in final_waits:
                        e.wait_ge(sems[s], v)

            deco(body)


def build_program(T, L, last_to_out=True):
    NT = T // TT
    NKT = T // 128
    nc = bass.Bass("TRN2", target_bir_lowering=False)
    dt = nc.dram_tensor

    x_in = dt("x", [T, D], F32, kind="ExternalInput").ap()
    cT_in = dt("cT", [128, 16], F32, kind="ExternalInput").ap()
    w_ada_in = dt("w_ada", [L, D, 3 * D], F32, kind="ExternalInput").ap()
    b_ada_in = dt("b_ada_t", [128, L * 48], F32, kind="ExternalInput").ap()
    g_pre_in = dt("g_pre_t", [128, L * 16], F32, kind="ExternalInput").ap()
    g_post_in = dt("g_post_t", [128, L * 16], F32, kind="ExternalInput").ap()
    w_in_in = dt("w_in", [L, D, INW], F32, kind="ExternalInput").ap()
    conv_w_in = dt("conv_w_t", [128, L * 8 * KCONV], F32, kind="ExternalInput").ap()
    conv_b_in = dt("conv_b_t", [128, L * 8], F32, kind="ExternalInput").ap()
    cn_g_in = dt("cn_g_t", [128, L * 8], F32, kind="ExternalInput").ap()
    cn_b_in = dt("cn_b_t", [128, L * 8], F32, kind="ExternalInput").ap()
    w_co_in = dt("w_conv_out", [L, CW, D], F32, kind="ExternalInput").ap()
    lam_in = dt("lam_t", [128, L * 4 * HD], F32, kind="ExternalInput").ap()
    subln_in = dt("subln_t", [128, L], F32, kind="ExternalInput").ap()
    w_ao_in = dt("w_attn_out", [L, CW, D], F32, kind="ExternalInput").ap()
    w_o_in = dt("w_o", [L, D, D], F32, kind="ExternalInput").ap()
    ident_in = dt("ident", [128, 128], F32, kind="ExternalInput").ap()
    trimask_in = dt("trimask", [128, 128], F32, kind="ExternalInput").ap()
    kaug_in = dt("kaug", [128, 2048], F32, kind="ExternalInput").ap()
    qaug_in = dt("qaug", [128, NH * 512], F32, kind="ExternalInput").ap()
    btab_in = dt("btab", [128, NH * NREL], F32, kind="ExternalInput").ap()

    out_d = dt("out", [T, D], F32, kind="ExternalOutput").ap()

    xT_d = dt("xT_s", [D, T], F32, kind="Internal").ap()
    wi_d = dt("wi_s", [L * NBLK_IN * 128, 16 * 512], BF16, kind="Internal").ap()
    wco_d = dt("wco_s", [L * 4 * 128, 8 * 512], BF16, kind="Internal").ap()
    wao_d = dt("wao_s", [L * 4 * 128, 8 * 512], BF16, kind="Internal").ap()
    wo_d = dt("wo_s", [L * 4 * 128, 16 * 512], BF16, kind="Internal").ap()
    KT_d = dt("KT_s", [NH * 128, T], BF16, kind="Internal").ap()
    Vh_d = dt("Vh_s", [NH * 128, NKT * 128], BF16, kind="Internal").ap()

    P = Prog(nc)
    sb = nc.alloc_sbuf_tensor
    A32 = sb("A32", [128, 16 * 512], F32).ap()
    U32 = sb("U32", [128, 8 * 544], F32).ap()
    XIN = sb("XIN", [128, 4 * 512], F32).ap()
    HT = sb("HT", [128, 16 * 512], BF16).ap()
    WB = sb("WB", [128, 2 * 8192], BF16).ap()
    SCZ = sb("SCZ", [128, 8 * 512], BF16).ap()
    QT = sb("QT", [128, 16 * 512], BF16).ap()
    AZ = sb("AZ", [128, 8 * 512], BF16).ap()
    MK = sb("MK", [128, 16 * 512], BF16).ap()
    VB = sb("VB", [128, 4 * 1024], BF16).ap()
    EB = sb("EB", [128, 3 * 1024], BF16).ap()
    TF = sb("TF", [128, 6 * 512], F32).ap()
    TB = sb("TB", [128, 4 * 512], BF16).ap()
    STG = sb("STG", [128, 6 * 512], BF16).ap()
    HALO = sb("HALO", [128, 8 * 32], F32).ap()
    ONES = sb("ONES", [128, 128], BF16).ap()
    IDENT = sb("IDENT", [128, 128], F32).ap()
    TRI = sb("TRI", [128, 128], BF16).ap()
    KAUG = sb("KAUGC", [128, 2048], BF16).ap()
    BTAB = sb("BTAB", [128, NH * NREL], F32).ap()
    CACT = sb("CACT", [128, 16], F32).ap()
    MOD = sb("MOD", [128, L * 48], F32).ap()
    GM = sb("GM", [128, L * 16], F32).ap()
    GG = sb("GG", [128, L * 16], F32).ap()
    GPRE = sb("GPRE", [128, L * 16], F32).ap()
    GPOST = sb("GPOST", [128, L * 16], F32).ap()
    BADA = sb("BADA", [128, L * 48], F32).ap()
    CONVW = sb("CONVW", [128, L * 8 * KCONV], F32).ap()
    CONVB = sb("CONVB", [128, L * 8], F32).ap()
    CNG = sb("CNG", [128, L * 8], F32).ap()
    CNB = sb("CNB", [128, L * 8], F32).ap()
    LAMS = sb("LAMS", [128, 8], F32).ap()
    NLAM = sb("NLAM", [128, L], F32).ap()
    GSUB = sb("GSUB", [128, L], F32).ap()
    EPSC_T = sb("EPSC", [128, 1], F32).ap()
    EPSC = EPSC_T[:, 0:1]
    LAMV = TF[:, 1024:1024 + L * 4 * HD]
    LAMT = TF[:, 2048:2048 + 2 * HD]
    PS = nc.alloc_psum_tensor("PS", [128, 8 * 512], F32).ap()

    def ps(bank, n=512, off=0):
        return PS[:, bank * 512 + off: bank * 512 + off + n]

    def tf(i):
        return TF[:, i * 512:(i + 1) * 512]

    def tb(i):
        return TB[:, i * 512:(i + 1) * 512]

    ctr = {}

    def rr(name, n):
        v = ctr.get(name, 0)
        ctr[name] = v + 1
        return v % n

    def dma(q, out, in_, reads, writes, chan):
        P.add(q, lambda e, out=out, in_=in_: e.dma_start(out=out, in_=in_), reads, writes, chan=chan)

    def act(out, in_, func, reads, writes, bias=None, scale=None):
        kw = {}
        if bias is not None:
            kw["bias"] = bias
        if scale is not None:
            kw["scale"] = scale
        P.add("act", lambda e, out=out, in_=in_, func=func, kw=kw: e.activation(out, in_, func, **kw),
              reads, writes)

    def tt(out, in0, in1, op, reads, writes, eng="dve"):
        P.add(eng, lambda e, out=out, in0=in0, in1=in1, op=op: e.tensor_tensor(out, in0, in1, op),
              reads, writes)

    def ts(out, in0, s1, s2, op0, op1, reads, writes, eng="dve"):
        if op1 is None:
            P.add(eng, lambda e, out=out, in0=in0, s1=s1, op0=op0: e.tensor_scalar(out, in0, s1, None, op0),
                  reads, writes)
        else:
            P.add(eng, lambda e, out=out, in0=in0, s1=s1, s2=s2, op0=op0, op1=op1:
                  e.tensor_scalar(out, in0, s1, s2, op0, op1), reads, writes)

    def recip(out, in_, reads, writes):
        P.add("dve", lambda e, out=out, in_=in_: e.reciprocal(out, in_), reads, writes)

    def stt(out, in0, sc, in1, op0, op1, reads, writes, eng="dve"):
        P.add(eng, lambda e, out=out, in0=in0, sc=sc, in1=in1, op0=op0, op1=op1:
              e.scalar_tensor_tensor(out, in0, sc, in1, op0, op1), reads, writes)

    def mm(out, lhsT, rhs, start, stop, reads, writes):
        P.add("pe", lambda e, out=out, lhsT=lhsT, rhs=rhs, start=start, stop=stop:
              e.matmul(out, lhsT, rhs, start=start, stop=stop), reads, writes)

    def tr(out, in_, reads, writes):
        P.add("pe", lambda e, out=out, in_=in_: e.transpose(out, in_, IDENT), reads, writes)

    dma("sp", IDENT, ident_in, (), (("IDENT",),), "c0")
    dma("sp", BTAB, btab_in, (), (("BTAB",),), "c1")
    dma("sp", tf(0)[:, 0:128], trimask_in, (), (("TF", 0),), "c2")
    act(TRI, tf(0)[:, 0:128], AF.Copy, (("TF", 0),), (("TRI",),))
    P.add("dve", lambda e: e.memset(ONES, 1.0), (), (("ONES",),))
    P.add("dve", lambda e: e.memset(EPSC_T, EPS), (), (("EPSC",),))
    P.add("dve", lambda e: e.memset(HALO, 0.0), (), [("HALO", c) for c in range(8)])
    dma("sp", A32[:, 0:2048], kaug_in, (), [("A32", k) for k in range(4)], "A32")
    act(KAUG[64:67, :], A32[64:67, 0:2048], AF.Copy, [("A32", k) for k in range(4)], (("KAUG",),))
    dma("sp", A32[:, 4096:8192], qaug_in, (), [("A32", k) for k in range(8, 16)], "c13")
    for h in range(NH):
        for m in range(2):
            act(QT[64:67, (2 * h + m) * 512:(2 * h + m + 1) * 512], A32[64:67, 4096 + h * 512:4096 + (h + 1) * 512],
                AF.Copy, [("A32", k) for k in range(8, 16)], (("QTaug", h, m),))
    for (dst, src, key, ch) in ((GPRE, g_pre_in, "GPRE", "c3"), (GPOST, g_post_in, "GPOST", "c4"),
                                (BADA, b_ada_in, "BADA", "c5"), (CONVW, conv_w_in, "CONVW", "c6"),
                                (CONVB, conv_b_in, "CONVB", "c7"), (CNG, cn_g_in, "CNG", "c8"),
                                (CNB, cn_b_in, "CNB", "c9"), (LAMV, lam_in, "LAMVX", "c10"),
                                (GSUB, subln_in, "GSUBraw", "c11"), (CACT, cT_in, "CACTraw", "c12")):
        dma("sp", dst, src, (), ((key,),) if key != "LAMVX" else (("TF", 2), ("TF", 3)), ch)
    act(CACT, CACT, AF.Silu, (("CACTraw",),), (("CACT",),))
    for l in range(L):
        lam0 = lambda_init(l)
        b0 = l * 4 * HD
        tt(LAMT[:, 0:HD], LAMV[:, b0:b0 + HD], LAMV[:, b0 + HD:b0 + 2 * HD], ALU.mult, (("TF", 2), ("TF", 3)), (("TF", 4),))
        tt(LAMT[:, HD:2 * HD], LAMV[:, b0 + 2 * HD:b0 + 3 * HD], LAMV[:, b0 + 3 * HD:b0 + 4 * HD], ALU.mult,
           (("TF", 2), ("TF", 3), ("TF", 4)), (("TF", 4),))
        P.add("dve", lambda e: e.reduce_sum(LAMS[:, 0:1], LAMT[:, 0:HD], mybir.AxisListType.X),
              (("TF", 4),), (("LAMS",),))
        P.add("dve", lambda e: e.reduce_sum(LAMS[:, 1:2], LAMT[:, HD:2 * HD], mybir.AxisListType.X),
              (("TF", 4), ("LAMS",)), (("LAMS",),))
        act(LAMS[:, 2:4], LAMS[:, 0:2], AF.Exp, (("LAMS",),), (("LAMS",),))
        stt(NLAM[:, l:l + 1], LAMS[:, 3:4], -lam0, LAMS[:, 2:3], ALU.add, ALU.subtract,
            (("LAMS",),), (("NLAM", l),))
        ts(GSUB[:, l:l + 1], GSUB[:, l:l + 1], 1.0 - lam0, None, ALU.mult, None,
           (("GSUBraw",), ("GSUB", l - 1)), (("GSUB", l),))
    P.freeze(("EPSC",), ("IDENT",), ("BTAB",), ("TRI",), ("ONES",), ("KAUG",), ("CACT",), ("GPRE",), ("GPOST",), ("BADA",),
             ("CONVW",), ("CONVB",), ("CNG",), ("CNB",))

    for l in range(L):
        for grp in range(12):
            src = w_ada_in[l].rearrange("(kc p) n -> p kc n", p=128)[:, :, grp * 512:(grp + 1) * 512]
            dma("sp", A32.rearrange("p (kc n) -> p kc n", kc=16), src, (), [("A32", k) for k in range(16)], "A32")
            for s in range(4):
                oc = grp * 4 + s
                bank = rr("ps", 8)
                for kc in range(16):
                    mm(ps(bank, 1), A32[:, kc * 512 + s * 128: kc * 512 + (s + 1) * 128], CACT[:, kc:kc + 1],
                       kc == 0, kc == 15, [("A32", k) for k in range(16)] + [("CACT",)], (("ps", bank),))
                tt(MOD[:, l * 48 + oc: l * 48 + oc + 1], ps(bank, 1), BADA[:, l * 48 + oc: l * 48 + oc + 1], ALU.add,
                   (("ps", bank), ("BADA",)), (("MOD", l),))
        stt(GM[:, l * 16:(l + 1) * 16], MOD[:, l * 48 + 16:l * 48 + 32], 1.0, GPRE[:, l * 16:(l + 1) * 16],
            ALU.add, ALU.mult, (("MOD", l), ("GPRE",)), (("GM", l),))
        tt(GG[:, l * 16:(l + 1) * 16], MOD[:, l * 48 + 32:l * 48 + 48], GPOST[:, l * 16:(l + 1) * 16], ALU.mult,
           (("MOD", l), ("GPOST",)), (("GG", l),))

    cvn = [0]

    def convert(src_rows, dst_view, ncols):
        i = cvn[0]
        cvn[0] += 1
        s32 = i % 2
        sbf = i % 2
        stage = A32[:, s32 * 2048: s32 * 2048 + ncols]
        stb = HT[:, sbf * 2048: sbf * 2048 + ncols]
        dma("sp", stage, src_rows, (), (("A32", "cv", s32),), "cvl%d" % s32)
        if i % 2 == 0:
            act(stb, stage, AF.Copy, (("A32", "cv", s32),), (("HT", "cv", sbf),))
        else:
            P.add("dve", lambda e, stb=stb, stage=stage: e.tensor_copy(stb, stage),
                  (("A32", "cv", s32),), (("HT", "cv", sbf),))
        dma("pool", dst_view, stb.rearrange("p (b c) -> p b c", c=512), (("HT", "cv", sbf),),
            (("WSCRP", sbf),), "cvs%d" % sbf)

    P.add("dve", lambda e: e.memset(TF[:, 0:1], 0.0), (),
          [("A32", k) for k in range(16)] + [("A32", "cv", 0), ("A32", "cv", 1), ("TF", 0)])
    for l in range(L):
        wv = wi_d[l * NBLK_IN * 128:(l + 1) * NBLK_IN * 128, :].rearrange("(b p) c -> p b c", p=128)
        for kc in range(16):
            for pc in range(NBLK_IN // 2):
                convert(w_in_in[l, kc * 128:(kc + 1) * 128, pc * 1024:(pc + 1) * 1024],
                        wv[:, 2 * pc:2 * pc + 2, kc * 512:(kc + 1) * 512], 1024)
        for (src, dst, nkc) in ((w_co_in, wco_d, 8), (w_ao_in, wao_d, 8), (w_o_in, wo_d, 16)):
            wv2 = dst[l * 4 * 128:(l + 1) * 4 * 128, :].rearrange("(b p) c -> p b c", p=128)
            for kc in range(nkc):
                for pc in range(2):
                    convert(src[l, kc * 128:(kc + 1) * 128, pc * 1024:(pc + 1) * 1024],
                            wv2[:, 2 * pc:2 * pc + 2, kc * 512:(kc + 1) * 512], 1024)
    P.add("dve", lambda e: e.memset(TF[:, 0:1], 0.0), (),
          [("A32", "cv", 0), ("A32", "cv", 1), ("HT", "cv", 0), ("HT", "cv", 1), ("TF", 0)]
          + [("A32", k) for k in range(16)] + [("HT", k) for k in range(16)])

    def load_wblock(src_rows, nkc):
        slot = rr("wb", 2)
        view = WB[:, slot * 8192: slot * 8192 + nkc * 512]
        dma("sp", view, src_rows[:, 0:nkc * 512], (("WSCRP", 0), ("WSCRP", 1)), (("WB", slot),), "wb%d" % slot)
        return slot

    def wblk(slot, kc, s):
        return WB[:, slot * 8192 + kc * 512 + s * 128: slot * 8192 + kc * 512 + (s + 1) * 128]

    SPAN = 8
    NSLOT = 4
    HTall = [("HT", k) for k in range(16)]

    def wi_rows(l, blk):
        r0 = (l * NBLK_IN + blk) * 128
        return wi_d[r0:r0 + 128, :]

    def proj_fm(s, slot, bank):
        for kc in range(16):
            mm(ps(bank), wblk(slot, kc, s), HT[:, kc * 512:(kc + 1) * 512], kc == 0, kc == 15,
               (("WB", slot), ("HT", kc)), (("ps", bank),))

    def rstd_from(bank, scale, dst, dstk):
        act(dst, ps(bank), AF.Sqrt, (("ps", bank), ("EPSC",)), (dstk,), bias=EPSC, scale=scale)
        recip(dst, dst, (dstk,), (dstk,))

    def phaseA_first(l, i):
        t0 = i * TT
        for g in range(4):
            xs = rr("xtok", 2)
            xtok = U32[:, xs * 2176: xs * 2176 + 2048]
            xk = [("U32", c) for c in range(xs * 4, xs * 4 + 4)]
            dma("sp", xtok, x_in[t0 + g * 128: t0 + (g + 1) * 128, :], (), xk, "xtok%d" % xs)
            for kq in range(4):
                bank = rr("ps", 8)
                for j in range(4):
                    kc = kq * 4 + j
                    tr(ps(bank, 128, j * 128), xtok[:, kc * 128:(kc + 1) * 128], xk + [("IDENT",)], (("ps", bank),))
                dst = A32.rearrange("p (kc t) -> p kc t", kc=16)[:, kq * 4:kq * 4 + 4, g * 128:(g + 1) * 128]
                srcp = ps(bank).rearrange("p (j t) -> p j t", j=4)
                if (g + kq) % 2 == 0:
                    act(dst, srcp, AF.Copy, (("ps", bank),), [("A32", kq * 4 + j) for j in range(4)])
                else:
                    P.add("dve", lambda e, dst=dst, srcp=srcp: e.tensor_copy(dst, srcp), (("ps", bank),),
                          [("A32", kq * 4 + j) for j in range(4)])
        dma("pool", xT_d.rearrange("(kc p) t -> p kc t", p=128)[:, :, t0:t0 + TT],
            A32.rearrange("p (kc t) -> p kc t", kc=16), [("A32", k) for k in range(16)], (("xT", i),), "A32st")
        bank = rr("ps", 8)
        for kc in range(16):
            tbi = rr("tb", 4)
            act(tb(tbi), A32[:, kc * 512:(kc + 1) * 512], AF.Square, (("A32", kc),), (("TB", tbi),))
            mm(ps(bank), ONES, tb(tbi), kc == 0, kc == 15, (("TB", tbi), ("ONES",)), (("ps", bank),))
        rstd_from(bank, 1.0 / D, tf(0), ("TF", 0))
        for kc in range(16):
            tfi = 1 + rr("tfA", 2)
            tt(tf(tfi), A32[:, kc * 512:(kc + 1) * 512], tf(0), ALU.mult, (("A32", kc), ("TF", 0)), (("TF", tfi),))
            act(HT[:, kc * 512:(kc + 1) * 512], tf(tfi), AF.Identity, (("TF", tfi), ("GM", l), ("MOD", l)),
                (("HT", kc),), bias=MOD[:, l * 48 + kc: l * 48 + kc + 1], scale=GM[:, l * 16 + kc: l * 16 + kc + 1])

    def phaseA_stream(l, i):
        t0 = i * TT
        bank = rr("ps", 8)
        for kc in range(16):
            xs = rr("xin", 4)
            xin = XIN[:, xs * 512:(xs + 1) * 512]
            dma("sp", xin, xT_d[kc * 128:(kc + 1) * 128, t0:t0 + TT], (("xT", i),), (("XIN", xs),), "xin%d" % xs)
            tbi = rr("tb", 4)
            act(tb(tbi), xin, AF.Square, (("XIN", xs),), (("TB", tbi),))
            mm(ps(bank), ONES, tb(tbi), kc == 0, kc == 15, (("TB", tbi), ("ONES",)), (("ps", bank),))
        rstd_from(bank, 1.0 / D, tf(5), ("TF", 5))
        for kc in range(16):
            xs = rr("xin", 4)
            xin = XIN[:, xs * 512:(xs + 1) * 512]
            dma("sp", xin, xT_d[kc * 128:(kc + 1) * 128, t0:t0 + TT], (("xT", i),), (("XIN", xs),), "xin%d" % xs)
            tt(xin, xin, tf(5), ALU.mult, (("XIN", xs), ("TF", 5)), (("XIN", xs),))
            act(HT[:, kc * 512:(kc + 1) * 512], xin, AF.Identity, (("XIN", xs), ("GM", l), ("MOD", l)),
                (("HT", kc),), bias=MOD[:, l * 48 + kc: l * 48 + kc + 1], scale=GM[:, l * 16 + kc: l * 16 + kc + 1])

    def conv_chunk(l, i, c):
        ub = c * 544
        if i == 0:
            P.add("dve", lambda e, ub=ub: e.memset(U32[:, ub:ub + 30], 0.0), (("U32", c),), (("U32", c),))
        else:
            P.add("dve", lambda e, ub=ub, c=c: e.tensor_copy(U32[:, ub:ub + 30], HALO[:, c * 32:c * 32 + 30]),
                  (("HALO", c), ("U32", c)), (("U32", c),))
        P.add("dve", lambda e, ub=ub, c=c: e.tensor_copy(HALO[:, c * 32:c * 32 + 30], U32[:, ub + 512:ub + 542]),
              (("U32", c),), (("HALO", c),))
        wb0 = (l * 8 + c) * KCONV
        accs = (tf(1), tf(2))
        acck = (("TF", 1), ("TF", 2))
        ts(accs[0], U32[:, ub:ub + 512], CONVW[:, wb0:wb0 + 1], CONVB[:, l * 8 + c:l * 8 + c + 1], ALU.mult, ALU.add,
           (("U32", c), ("CONVW",), ("CONVB",)), (acck[0],))
        ts(accs[1], U32[:, ub + 1:ub + 513], CONVW[:, wb0 + 1:wb0 + 2], None, ALU.mult, None,
           (("U32", c), ("CONVW",)), (acck[1],))
        for j in range(2, KCONV):
            a = j % 2
            stt(accs[a], U32[:, ub + j:ub + j + 512], CONVW[:, wb0 + j:wb0 + j + 1], accs[a], ALU.mult, ALU.add,
                (("U32", c), ("CONVW",), acck[a]), (acck[a],))
        tt(U32[:, ub + 30:ub + 542], accs[0], accs[1], ALU.add, (acck[0], acck[1]), (("U32", c),))

    def phaseBE(l, i):
        t0 = i * TT
        for q4 in range(4):
            act(MK[64:67, q4 * 2048:(q4 + 1) * 2048], KAUG[64:67, :], AF.Copy,
                [("MK", q4 * 4 + j) for j in range(4)] + [("KAUG",)], [("MK", q4 * 4 + j) for j in range(4)])
        for blk in range(14):
            slot = load_wblock(wi_rows(l, blk), 16)
            if blk in (10, 11):
                for g in range(4):
                    bank = rr("ps", 8)
                    for kc in range(16):
                        mm(ps(bank), HT[:, kc * 512 + g * 128: kc * 512 + (g + 1) * 128],
                           WB[:, slot * 8192 + kc * 512: slot * 8192 + (kc + 1) * 512], kc == 0, kc == 15,
                           (("WB", slot), ("HT", kc)), (("ps", bank),))
                    st = rr("stg", 6)
                    stg = STG[:, st * 512:(st + 1) * 512]
                    act(stg, ps(bank), AF.Copy, (("ps", bank),), (("STG", st),))
                    kt = i * 4 + g
                    h0 = (blk - 10) * 4
                    dstv = Vh_d.rearrange("(h p) (n d) -> p h n d", p=128, d=128)[:, h0:h0 + 4, kt, :]
                    dma("pool", dstv, stg.rearrange("p (h d) -> p h d", h=4), (("STG", st),),
                        [("Vh", h0 + hh, i, g) for hh in range(4)], "stg%d" % st)
                continue
            for s in range(4):
                bank = rr("ps", 8)
                proj_fm(s, slot, bank)
                c = (blk % 2) * 4 + s
                if blk in (0, 1):
                    act(U32[:, c * 544 + 30: c * 544 + 542], ps(bank), AF.Copy, (("ps", bank),), (("U32", c),))
                elif blk in (2, 3):
                    tfi = 3 + rr("tfB", 2)
                    act(tf(tfi), ps(bank), AF.Sigmoid, (("ps", bank),), (("TF", tfi),))
                    tt(U32[:, c * 544 + 30: c * 544 + 542], U32[:, c * 544 + 30: c * 544 + 542], tf(tfi), ALU.mult,
                       (("U32", c), ("TF", tfi)), (("U32", c),))
                elif blk in (4, 5):
                    act(SCZ[:, c * 512:(c + 1) * 512], ps(bank), AF.Silu, (("ps", bank),), (("SCZ", c),))
                elif blk in (6, 7):
                    act(QT[0:64, (2 * c) * 512:(2 * c + 1) * 512], ps(bank)[0:64, :], AF.Identity, (("ps", bank),),
                        (("QT", c, 0),), scale=HD ** -0.5)
                    st = rr("stg", 6)
                    stg = STG[:, st * 512:(st + 1) * 512]
                    act(stg[64:128, :], ps(bank)[64:128, :], AF.Identity, (("ps", bank),), (("STG", st),),
                        scale=HD ** -0.5)
                    dma("pool", QT[0:64, (2 * c + 1) * 512:(2 * c + 2) * 512], stg[64:128, :], (("STG", st),),
                        (("QT", c, 1),), "stg%d" % st)
                elif blk in (8, 9):
                    st = rr("stg", 6)
                    stg = STG[:, st * 512:(st + 1) * 512]
                    act(stg, ps(bank), AF.Copy, (("ps", bank),), (("STG", st),))
                    dma("pool", KT_d[c * 128:(c + 1) * 128, t0:t0 + TT], stg, (("STG", st),), (("KT", c, i),),
                        "stg%d" % st)
                elif blk in (12, 13):
                    act(AZ[:, c * 512:(c + 1) * 512], ps(bank), AF.Silu, (("ps", bank),), (("AZ", c),))
            if blk == 3:
                for c in range(8):
                    conv_chunk(l, i, c)

        bank_m = rr("ps", 8)
        bank_q = rr("ps", 8)
        for c in range(8):
            ub = c * 544
            t1 = rr("tb", 4)
            act(tb(t1), U32[:, ub + 30:ub + 542], AF.Copy, (("U32", c),), (("TB", t1),))
            mm(ps(bank_m), ONES, tb(t1), c == 0, c == 7, (("TB", t1), ("ONES",)), (("ps", bank_m),))
            t2 = rr("tb", 4)
            act(tb(t2), U32[:, ub + 30:ub + 542], AF.Square, (("U32", c),), (("TB", t2),))
            mm(ps(bank_q), ONES, tb(t2), c == 0, c == 7, (("TB", t2), ("ONES",)), (("ps", bank_q),))
        ts(tf(3), ps(bank_m), 1.0 / CW, None, ALU.mult, None, (("ps", bank_m),), (("TF", 3),))
        tt(tf(5), tf(3), tf(3), ALU.mult, (("TF", 3),), (("TF", 5),))
        stt(tf(4), ps(bank_q), 1.0 / CW, tf(5), ALU.mult, ALU.subtract, (("ps", bank_q), ("TF", 5)), (("TF", 4),))
        act(tf(4), tf(4), AF.Sqrt, (("TF", 4), ("EPSC",)), (("TF", 4),), bias=EPSC, scale=1.0)
        recip(tf(4), tf(4), (("TF", 4),), (("TF", 4),))
        def ln_apply(c):
            ub = c * 544
            tt(tf(0), U32[:, ub + 30:ub + 542], tf(3), ALU.subtract, (("U32", c), ("TF", 3)), (("TF", 0),))
            tt(tf(0), tf(0), tf(4), ALU.mult, (("TF", 0), ("TF", 4)), (("TF", 0),))
            act(tf(0), tf(0), AF.Silu, (("TF", 0), ("CNG",), ("CNB",)), (("TF", 0),),
                bias=CNB[:, l * 8 + c:l * 8 + c + 1], scale=CNG[:, l * 8 + c:l * 8 + c + 1])
            tt(SCZ[:, c * 512:(c + 1) * 512], tf(0), SCZ[:, c * 512:(c + 1) * 512], ALU.mult,
               (("TF", 0), ("SCZ", c)), (("SCZ", c),))

        nkt = 4 * (i + 1)
        nspan = (nkt + SPAN - 1) // SPAN
        for h in range(NH):
            blocks = []
            for sp in range(nspan):
                ntl = min(SPAN, nkt - sp * SPAN)
                for ktl in range(ntl):
                    blocks.append((sp, ktl, ntl))
            cur = {}

            def issue_loads(sp, ntl):
                sl = rr("kb", NSLOT)
                cur[sp] = sl
                tl0 = (sp * SPAN) // 4
                tiles = range(tl0, tl0 + (ntl + 3) // 4)
                kkeys = [("MK", sl * 4 + j) for j in range(4)]
                for m in range(2):
                    base = sl * 2048 + m * 1024
                    dma("sp", MK[0:64, base: base + ntl * 128],
                        KT_d[h * 128 + m * 64: h * 128 + (m + 1) * 64, sp * SPAN * 128: sp * SPAN * 128 + ntl * 128],
                        [("KT", h, tq) for tq in tiles], [("MK", sl * 4 + 2 * m), ("MK", sl * 4 + 2 * m + 1)],
                        "kb%d%d" % (sl, m))
                dma("sp", VB[:, sl * 1024: sl * 1024 + ntl * 128],
                    Vh_d[h * 128:(h + 1) * 128, sp * SPAN * 128: sp * SPAN * 128 + ntl * 128],
                    [("Vh", h, tq, g) for tq in tiles for g in range(4)], (("VB", sl),), "vb%d" % sl)

            def qk_exp(bi):
                sp, ktl, ntl = blocks[bi]
                if ktl == 0:
                    issue_loads(sp, ntl)
                sl = cur[sp]
                kt = sp * SPAN + ktl
                mdiag = kt - 4 * i
                c0 = 128 * max(mdiag, 0)
                n = 512 - c0
                rel = 4 * i - kt
                sb2 = bi % 2
                eb = bi % 3
                for m in range(2):
                    base = sl * 2048 + m * 1024
                    mm(ps(sb2 * 2 + m, n, c0), MK[0:67, base + ktl * 128: base + (ktl + 1) * 128],
                       QT[0:67, (2 * h + m) * 512 + c0:(2 * h + m + 1) * 512], True, True,
                       [("MK", sl * 4 + 2 * m), ("MK", sl * 4 + 2 * m + 1), ("QT", h, m), ("QTaug", h, m)],
                       (("ps", sb2 * 2 + m),))
                ein = PS[:, sb2 * 1024:(sb2 + 1) * 1024].rearrange("p (m q) -> p m q", m=2)[:, :, c0:512]
                eout = EB[:, eb * 1024:(eb + 1) * 1024].rearrange("p (m q) -> p m q", m=2)[:, :, c0:512]
                act(eout, ein, AF.Exp, (("ps", sb2 * 2), ("ps", sb2 * 2 + 1), ("BTAB",)), (("EB", eb),),
                    bias=BTAB[:, h * NREL + rel + 8: h * NREL + rel + 9], scale=1.0)
                if mdiag >= 0:
                    for m in range(2):
                        blkv = EB[:, eb * 1024 + m * 512 + c0: eb * 1024 + m * 512 + c0 + 128]
                        tt(blkv, blkv, TRI, ALU.mult, (("EB", eb), ("TRI",)), (("EB", eb),))

            def pv(bi):
                sp, ktl, ntl = blocks[bi]
                sl = cur[sp]
                kt = sp * SPAN + ktl
                c0 = 128 * max(kt - 4 * i, 0)
                n = 512 - c0
                eb = bi % 3
                first = (kt == 0)
                last = (kt == nkt - 1)
                for m in range(2):
                    erhs = EB[:, eb * 1024 + m * 512 + c0: eb * 1024 + (m + 1) * 512]
                    mm(ps(4 + m, n, c0), VB[:, sl * 1024 + ktl * 128: sl * 1024 + (ktl + 1) * 128], erhs,
                       first, last, (("VB", sl), ("EB", eb)), (("ps", 4 + m),))
                    mm(ps(6 + m, n, c0), ONES, erhs, first, last, (("EB", eb), ("ONES",)), (("ps", 6 + m),))

            nb = len(blocks)
            for bi in range(nb + 1):
                if bi < nb:
                    qk_exp(bi)
                if bi >= 1:
                    pv(bi - 1)
            ln_apply(h)
            oh = U32[:, h * 544: h * 544 + 512]
            recip(tf(1), ps(6), (("ps", 6),), (("TF", 1),))
            recip(tf(2), ps(7), (("ps", 7),), (("TF", 2),))
            tt(tf(1), ps(4), tf(1), ALU.mult, (("ps", 4), ("TF", 1)), (("TF", 1),))
            tt(tf(2), ps(5), tf(2), ALU.mult, (("ps", 5), ("TF", 2)), (("TF", 2),))
            stt(oh, tf(2), NLAM[:, l:l + 1], tf(1), ALU.mult, ALU.add, (("TF", 1), ("TF", 2), ("NLAM", l)),
                (("U32", h),))
        for h in range(NH):
            oh = U32[:, h * 544: h * 544 + 512]
            t1 = rr("tb", 4)
            act(tb(t1), oh, AF.Square, (("U32", h),), (("TB", t1),))
            bank = rr("ps", 8)
            mm(ps(bank), ONES, tb(t1), True, True, (("TB", t1), ("ONES",)), (("ps", bank),))
            tfi = 1 + (h % 2)
            rstd_from(bank, 1.0 / VD, tf(tfi), ("TF", tfi))
            tt(oh, oh, tf(tfi), ALU.mult, (("U32", h), ("TF", tfi)), (("U32", h),))
            stt(AZ[:, h * 512:(h + 1) * 512], oh, GSUB[:, l:l + 1], AZ[:, h * 512:(h + 1) * 512], ALU.mult,
                ALU.mult, (("U32", h), ("GSUB", l), ("AZ", h)), (("AZ", h),))

        GC = U32[:, 0:2048]
        GA = U32[:, 2176:2176 + 2048]
        gck = [("U32", c) for c in range(4)]
        gak = [("U32", c) for c in range(4, 8)]
        for ob in range(4):
            slot = load_wblock(wi_rows(l, 14 + ob), 16)
            for s in range(4):
                bank = rr("ps", 8)
                proj_fm(s, slot, bank)
                act(GC[:, s * 512:(s + 1) * 512], ps(bank), AF.Sigmoid, (("ps", bank),), gck)
            slot = load_wblock(wi_rows(l, 18 + ob), 16)
            for s in range(4):
                bank = rr("ps", 8)
                proj_fm(s, slot, bank)
                act(GA[:, s * 512:(s + 1) * 512], ps(bank), AF.Sigmoid, (("ps", bank),), gak)
            r0 = (l * 4 + ob) * 128
            slot = load_wblock(wco_d[r0:r0 + 128, :], 8)
            for s in range(4):
                bank = rr("ps", 8)
                for kc in range(8):
                    mm(ps(bank), wblk(slot, kc, s), SCZ[:, kc * 512:(kc + 1) * 512], kc == 0, kc == 7,
                       (("WB", slot), ("SCZ", kc)), (("ps", bank),))
                tt(GC[:, s * 512:(s + 1) * 512], ps(bank), GC[:, s * 512:(s + 1) * 512], ALU.mult,
                   [("ps", bank)] + gck, gck)
            slot = load_wblock(wao_d[r0:r0 + 128, :], 8)
            for s in range(4):
                bank = rr("ps", 8)
                for kc in range(8):
                    mm(ps(bank), wblk(slot, kc, s), AZ[:, kc * 512:(kc + 1) * 512], kc == 0, kc == 7,
                       (("WB", slot), ("AZ", kc)), (("ps", bank),))
                tt(GA[:, s * 512:(s + 1) * 512], ps(bank), GA[:, s * 512:(s + 1) * 512], ALU.mult,
                   [("ps", bank)] + gak, gak)
                oc = ob * 4 + s
                tt(MK[:, oc * 512:(oc + 1) * 512], GC[:, s * 512:(s + 1) * 512], GA[:, s * 512:(s + 1) * 512],
                   ALU.add, gck + gak, (("MK", oc),))

    def phaseF(l, i):
        t0 = i * TT
        lastl = (l == L - 1)
        bank_o = rr("ps", 8)
        for ob in range(4):
            r0 = (l * 4 + ob) * 128
            slot = load_wblock(wo_d[r0:r0 + 128, :], 16)
            for s in range(4):
                oc = ob * 4 + s
                bank = rr("ps", 8)
                if bank == bank_o:
                    bank = rr("ps", 8)
                for kc in range(16):
                    mm(ps(bank), wblk(slot, kc, s), MK[:, kc * 512:(kc + 1) * 512], kc == 0, kc == 15,
                       (("WB", slot), ("MK", kc)), (("ps", bank),))
                act(A32[:, oc * 512:(oc + 1) * 512], ps(bank), AF.Copy, (("ps", bank),), (("A32", oc),))
                t1 = rr("tb", 4)
                act(tb(t1), ps(bank), AF.Square, (("ps", bank),), (("TB", t1),))
                mm(ps(bank_o), ONES, tb(t1), oc == 0, oc == 15, (("TB", t1), ("ONES",)), (("ps", bank_o),))
        rstd_from(bank_o, 1.0 / D, tf(0), ("TF", 0))
        for oc in range(16):
            xs = rr("xin", 4)
            xin = XIN[:, xs * 512:(xs + 1) * 512]
            dma("sp", xin, xT_d[oc * 128:(oc + 1) * 128, t0:t0 + TT], (("xT", i),), (("XIN", xs),), "xin%d" % xs)
            tt(A32[:, oc * 512:(oc + 1) * 512], A32[:, oc * 512:(oc + 1) * 512], tf(0), ALU.mult,
               (("A32", oc), ("TF", 0)), (("A32", oc),))
            stt(A32[:, oc * 512:(oc + 1) * 512], A32[:, oc * 512:(oc + 1) * 512], GG[:, l * 16 + oc:l * 16 + oc + 1],
                xin, ALU.mult, ALU.add, (("A32", oc), ("GG", l), ("XIN", xs)), (("A32", oc),))
        if not lastl:
            dma("pool", xT_d.rearrange("(kc p) t -> p kc t", p=128)[:, :, t0:t0 + TT],
                A32.rearrange("p (kc t) -> p kc t", kc=16), [("A32", k) for k in range(16)],
                (("xT", i),), "A32st")
        else:
            for g in range(4):
                osl = rr("ost", 2)
                ost = U32[:, osl * 2176: osl * 2176 + 2048]
                ostk = [("U32", c) for c in range(osl * 4, osl * 4 + 4)]
                for kq in range(4):
                    bank = rr("ps", 8)
                    for j in range(4):
                        kc = kq * 4 + j
                        tr(ps(bank, 128, j * 128), A32[:, kc * 512 + g * 128: kc * 512 + (g + 1) * 128],
                           (("A32", kc), ("IDENT",)), (("ps", bank),))
                    if kq % 2 == 0:
                        act(ost[:, kq * 512:(kq + 1) * 512], ps(bank), AF.Copy, (("ps", bank),), ostk)
                    else:
                        P.add("dve", lambda e, ost=ost, kq=kq, bank=bank:
                              e.tensor_copy(ost[:, kq * 512:(kq + 1) * 512], ps(bank)), (("ps", bank),), ostk)
                dma("pool", out_d[t0 + g * 128: t0 + (g + 1) * 128, :], ost, ostk, (("OUT",),), "ost%d" % osl)

    steps = [(l, i) for l in range(L) for i in range(NT)]
    for sidx, (l, i) in enumerate(steps):
        if l == 0:
            phaseA_first(l, i)
        elif sidx == 0:
            phaseA_stream(l, i)
        phaseBE(l, i)
        if sidx + 1 < len(steps) and steps[sidx + 1][0] >= 1:
            phaseA_stream(*steps[sidx + 1])
        phaseF(l, i)

    cnt = P.finalize()
    sems = {s: nc.alloc_semaphore(name="s_" + s) for s in P.streams}
    final_waits = [(s, 16 * cnt[s]) for s in ("ost0", "ost1") if s in cnt]
    with nc.Block() as block:
        P.emit_all(block, sems, final_waits)
    return nc, len(P.ops)


_ALIBI = [2.0 ** (-8.0 * (h + 1) / NH) for h in range(NH)]


def host_constants():
    ident = np.eye(128, dtype=np.float32)
    kk = np.arange(128)
    trimask = (kk[:, None] <= kk[None, :]).astype(np.float32)
    kaug = np.zeros((128, 2048), np.float32)
    kaug[64, :] = 1.0
    kaug[65, :] = 1.0
    kaug[66, :] = np.arange(2048) % 128
    qaug = np.zeros((128, NH * 512), np.float32)
    ii = np.arange(512)
    for h in range(NH):
        s = _ALIBI[h]
        qaug[64, h * 512:(h + 1) * 512] = -s * (ii % 256)
        qaug[65, h * 512:(h + 1) * 512] = -s * 256.0 * (ii // 256)
        qaug[66, h * 512:(h + 1) * 512] = s
    btab = np.zeros((128, NH * NREL), np.float32)
    for h in range(NH):
        for r in range(NREL):
            btab[:, h * NREL + r] = -_ALIBI[h] * 128.0 * (r - 8)
    return {"ident": ident, "trimask": trimask, "kaug": kaug, "qaug": qaug, "btab": btab}


def host_layout(L, c_b, w_ada, b_ada, g_pre, g_post, w_in, conv_w, conv_b, cn_g, cn_b, w_conv_out,
                lam_q1, lam_k1, lam_q2, lam_k2, subln_g, w_attn_out, w_o):
    f = np.float32
    m = {}
    m["cT"] = np.ascontiguousarray(c_b.reshape(16, 128).T).astype(f)
    m["w_ada"] = np.ascontiguousarray(w_ada[:L])
    m["b_ada_t"] = np.ascontiguousarray(b_ada[:L].reshape(L, 48, 128).transpose(2, 0, 1).reshape(128, L * 48))
    m["g_pre_t"] = np.ascontiguousarray(g_pre[:L].reshape(L, 16, 128).transpose(2, 0, 1).reshape(128, L * 16))
    m["g_post_t"] = np.ascontiguousarray(g_post[:L].reshape(L, 16, 128).transpose(2, 0, 1).reshape(128, L * 16))
    m["w_in"] = np.ascontiguousarray(w_in[:L])
    m["conv_w_t"] = np.ascontiguousarray(
        conv_w[:L].reshape(L, KCONV, 8, 128).transpose(3, 0, 2, 1).reshape(128, L * 8 * KCONV))
    for nm, a in (("conv_b_t", conv_b), ("cn_g_t", cn_g), ("cn_b_t", cn_b)):
        m[nm] = np.ascontiguousarray(a[:L].reshape(L, 8, 128).transpose(2, 0, 1).reshape(128, L * 8))
    m["w_conv_out"] = np.ascontiguousarray(w_conv_out[:L])
    lam = np.stack([lam_q1[:L], lam_k1[:L], lam_q2[:L], lam_k2[:L]], axis=1).reshape(1, L * 4 * HD)
    m["lam_t"] = np.ascontiguousarray(np.broadcast_to(lam, (128, L * 4 * HD)))
    m["subln_t"] = np.ascontiguousarray(subln_g[:L].T)
    m["w_attn_out"] = np.ascontiguousarray(w_attn_out[:L])
    m["w_o"] = np.ascontiguousarray(w_o[:L])
    return {k: np.asarray(v, dtype=f) for k, v in m.items()}


def kernel(x, c, w_ada, b_ada, g_pre, g_post, w_in, conv_w, conv_b, cn_g, cn_b, w_conv_out,
           lam_q1, lam_k1, lam_q2, lam_k2, subln_g, w_attn_out, w_o):
    arrs = [np.asarray(a, dtype=np.float32) for a in
            (x, c, w_ada, b_ada, g_pre, g_post, w_in, conv_w, conv_b, cn_g, cn_b, w_conv_out,
             lam_q1, lam_k1, lam_q2, lam_k2, subln_g, w_attn_out, w_o)]
    x, c = arrs[0], arrs[1]
    B, S, _ = x.shape
    L = arrs[2].shape[0]
    nc, _ = build_program(S, L)
    consts = host_constants()
    active = [0, 2, 4, 6][:B] if B <= 4 else list(range(B))
    in_maps = [None] * 8
    for b in range(B):
        mp = host_layout(L, c[b], *arrs[2:])
        mp.update(consts)
        mp["x"] = np.ascontiguousarray(x[b])
        in_maps[active[b]] = mp
    zmap = {k: np.zeros_like(v) for k, v in in_maps[active[0]].items()}
    for core in range(8):
        if in_maps[core] is None:
            in_maps[core] = zmap
    res = run_bass_kernel_spmd(nc, in_maps, core_ids=list(range(8)))
    out = np.stack([np.asarray(res.results[active[b]]["out"], dtype=np.float32) for b in range(B)], axis=0)
    return out
```
